# Optimizing a Trainium2 kernel written in Bass

```python
import math
import jax, jax.numpy as jnp
from jax import lax
import numpy as np

D_MODEL = 1024
BATCH = 4
SEQ = 4096
DEPTH = 4

PLE_DIM = 256
D_FF = 2816
EPS = 1e-6

GLA_HEADS = 4
GLA_DK = 64
GLA_DV = 128
GLA_KW = GLA_HEADS * GLA_DK
GLA_VW = GLA_HEADS * GLA_DV
GLA_GATE_RANK = 16
GLA_GATE_TAU = 16.0
GLA_CHUNK = 64
LRU_WIDTH = 512
LRU_BLOCKS = 8
LRU_BLOCK_W = LRU_WIDTH // LRU_BLOCKS
LRU_CONV_W = 4
LRU_C = 8.0
HYB_IN = 2 * GLA_KW + 2 * GLA_VW + GLA_GATE_RANK + 2 * LRU_WIDTH
HYB_MIX = GLA_VW + LRU_WIDTH

SWA_HEADS = 16
SWA_KV_HEADS = 4
SWA_HEAD_DIM = 64
SWA_GROUP = SWA_HEADS // SWA_KV_HEADS
SWA_WINDOW = 128
SWA_BLOCK = 128
SWA_QKV = (SWA_HEADS + 2 * SWA_KV_HEADS) * SWA_HEAD_DIM

REL_BUCKETS = 32
REL_MAX_DIST = 128

N_EVEN = (DEPTH + 1) // 2
N_ODD = DEPTH // 2

kernel_name = "hybrid_gla_rglru_swa_macaron"


def rms_norm(x, g):
    xf = x.astype(jnp.float32)
    y = xf * lax.rsqrt(jnp.mean(xf * xf, axis=-1, keepdims=True) + EPS)
    return (y * g.astype(jnp.float32)).astype(x.dtype)


def swiglu(x, w_gate, w_up, w_down):
    return (jax.nn.silu(x @ w_gate) * (x @ w_up)) @ w_down


def t5_bucket(dist):
    max_exact = REL_BUCKETS // 2
    d = jnp.maximum(dist, 1).astype(jnp.float32)
    large = max_exact + (jnp.log(d / max_exact) / math.log(REL_MAX_DIST / max_exact)
                         * (REL_BUCKETS - max_exact)).astype(jnp.int32)
    large = jnp.minimum(large, REL_BUCKETS - 1)
    return jnp.where(dist < max_exact, dist, large)


def gla(q, k, v, log_f, r, norm_g):
    B, S, _ = q.shape
    N = S // GLA_CHUNK
    C = GLA_CHUNK

    def split(t, d):
        return t.reshape(B, N, C, GLA_HEADS, d).transpose(0, 3, 1, 2, 4).astype(jnp.float32)

    qc = split(q, GLA_DK) * (GLA_DK ** -0.5)
    kc = split(k, GLA_DK)
    vc = split(v, GLA_DV)
    b = jnp.cumsum(split(log_f, GLA_DK), axis=3)
    b_last = b[..., -1:, :]
    q_dec = qc * jnp.exp(b)
    k_dec = kc * jnp.exp(-b)
    causal = jnp.tril(jnp.ones((C, C), dtype=bool))
    att = jnp.where(causal, jnp.einsum('bhnik,bhnjk->bhnij', q_dec, k_dec), 0.0)
    o_intra = jnp.einsum('bhnij,bhnjv->bhniv', att, vc)
    kv = jnp.einsum('bhnjk,bhnjv->bhnkv', kc * jnp.exp(b_last - b), vc)
    decay = jnp.exp(b_last[..., 0, :])

    def step(state, inp):
        d, u = inp
        return d[..., None] * state + u, state

    s0 = jnp.zeros((B, GLA_HEADS, GLA_DK, GLA_DV), jnp.float32)
    _, s_prev = lax.scan(step, s0, (jnp.moveaxis(decay, 2, 0), jnp.moveaxis(kv, 2, 0)))
    s_prev = jnp.moveaxis(s_prev, 0, 2)
    o = o_intra + jnp.einsum('bhnik,bhnkv->bhniv', q_dec, s_prev)
    o = o * lax.rsqrt(jnp.mean(o * o, axis=-1, keepdims=True) + EPS)
    o = o.transpose(0, 2, 3, 1, 4).reshape(B, S, GLA_VW)
    o = o * norm_g.astype(jnp.float32) * jax.nn.silu(r.astype(jnp.float32))
    return o.astype(v.dtype)


def causal_dwconv(x, w, b):
    S = x.shape[1]
    K = w.shape[0]
    xp = jnp.pad(x, ((0, 0), (K - 1, 0), (0, 0)))
    out = xp[:, 0:S] * w[0]
    for tap in range(1, K):
        out = out + xp[:, tap:tap + S] * w[tap]
    return out + b


def rg_lru(x, w_a, b_a, w_x, b_x, lam):
    B, S, _ = x.shape
    xf = x.astype(jnp.float32)
    xb = xf.reshape(B, S, LRU_BLOCKS, LRU_BLOCK_W)
    r = jax.nn.sigmoid(jnp.einsum('bsgi,gij->bsgj', xb, w_a.astype(jnp.float32)).reshape(B, S, LRU_WIDTH) + b_a)
    i = jax.nn.sigmoid(jnp.einsum('bsgi,gij->bsgj', xb, w_x.astype(jnp.float32)).reshape(B, S, LRU_WIDTH) + b_x)
    log_a = -LRU_C * r * jax.nn.softplus(-lam.astype(jnp.float32))
    a = jnp.exp(log_a)
    u = jnp.sqrt(-jnp.expm1(2.0 * log_a)) * (i * xf)

    def combine(left, right):
        a1, b1 = left
        a2, b2 = right
        return a1 * a2, a2 * b1 + b2

    _, h = lax.associative_scan(combine, (a, u), axis=1)
    return h.astype(x.dtype)


def hybrid_mixer(h, w_in, w_out, gla_w_fup, gla_b_f, gla_norm, conv_w, conv_b,
                 lru_w_a, lru_b_a, lru_w_x, lru_b_x, lru_lambda):
    z = h @ w_in
    sizes = (GLA_KW, GLA_KW, GLA_VW, GLA_VW, GLA_GATE_RANK, LRU_WIDTH, LRU_WIDTH)
    idx = [int(c) for c in np.cumsum(sizes)[:-1]]
    q, k, v, r, f_low, xr, gr = jnp.split(z, idx, axis=-1)
    log_f = jax.nn.log_sigmoid((f_low @ gla_w_fup + gla_b_f).astype(jnp.float32)) / GLA_GATE_TAU
    o_a = gla(q, k, v, log_f, r, gla_norm)
    xr = causal_dwconv(xr, conv_w, conv_b)
    o_b = rg_lru(xr, lru_w_a, lru_b_a, lru_w_x, lru_b_x, lru_lambda) * jax.nn.gelu(gr)
    return jnp.concatenate([o_a, o_b], axis=-1) @ w_out


def swa(h, w_qkv, b_qkv, w_o, b_o, sinks, rel_bias):
    B, S, _ = h.shape
    NB = S // SWA_BLOCK
    BLK = SWA_BLOCK
    z = h @ w_qkv + b_qkv
    q, k, v = jnp.split(z, [SWA_HEADS * SWA_HEAD_DIM, (SWA_HEADS + SWA_KV_HEADS) * SWA_HEAD_DIM], axis=-1)
    q = q.reshape(B, NB, BLK, SWA_KV_HEADS, SWA_GROUP, SWA_HEAD_DIM)
    k = k.reshape(B, S, SWA_KV_HEADS, SWA_HEAD_DIM)
    v = v.reshape(B, S, SWA_KV_HEADS, SWA_HEAD_DIM)

    def band(t):
        prev = jnp.pad(t, ((0, 0), (BLK, 0), (0, 0), (0, 0)))[:, :S]
        return jnp.concatenate([prev.reshape(B, NB, BLK, SWA_KV_HEADS, SWA_HEAD_DIM),
                                t.reshape(B, NB, BLK, SWA_KV_HEADS, SWA_HEAD_DIM)], axis=2)

    kb, vb = band(k), band(v)
    scores = jnp.einsum('bnqhgd,bnkhd->bnhgqk', q, kb).astype(jnp.float32) * (SWA_HEAD_DIM ** -0.5)
    qi = jnp.arange(BLK)[:, None] + BLK
    kj = jnp.arange(2 * BLK)[None, :]
    dist = qi - kj
    bias = rel_bias.astype(jnp.float32)[t5_bucket(jnp.maximum(dist, 0))]
    bias = bias.transpose(2, 0, 1).reshape(SWA_KV_HEADS, SWA_GROUP, BLK, 2 * BLK)
    key_pos = jnp.arange(NB)[:, None] * BLK - BLK + kj
    valid = (dist >= 0) & (dist < SWA_WINDOW) & (key_pos[:, None, :] >= 0)
    scores = jnp.where(valid[None, :, None, None], scores + bias, -1e30)
    sink = jnp.broadcast_to(sinks.astype(jnp.float32).reshape(1, 1, SWA_KV_HEADS, SWA_GROUP, 1, 1),
                            scores.shape[:-1] + (1,))
    probs = jax.nn.softmax(jnp.concatenate([scores, sink], axis=-1), axis=-1)[..., :-1]
    o = jnp.einsum('bnhgqk,bnkhd->bnqhgd', probs.astype(v.dtype), vb)
    o = o.reshape(B, S, SWA_HEADS * SWA_HEAD_DIM)
    return o @ w_o + b_o


def setup_inputs(seed: int = 0) -> dict:
    key = jax.random.key(seed)
    ks = iter(jax.random.split(key, 48))

    def nrm(shape, scale):
        return jax.random.normal(next(ks), shape, jnp.float32) * scale

    def gain(shape):
        return 1.0 + nrm(shape, 0.02)

    a_init = jax.random.uniform(next(ks), (N_EVEN, LRU_WIDTH), jnp.float32, 0.9, 0.999)
    return {
        "x": nrm((BATCH, SEQ, D_MODEL), 1.0),
        "p": nrm((DEPTH, BATCH, SEQ, PLE_DIM), 1.0),
        "rel_bias": nrm((REL_BUCKETS, SWA_HEADS), 0.5),
        "final_norm": gain((D_MODEL,)),
        "ffn1_norm": gain((DEPTH, D_MODEL)),
        "ffn1_w_gate": nrm((DEPTH, D_MODEL, D_FF), D_MODEL ** -0.5),
        "ffn1_w_up": nrm((DEPTH, D_MODEL, D_FF), D_MODEL ** -0.5),
        "ffn1_w_down": nrm((DEPTH, D_FF, D_MODEL), D_FF ** -0.5),
        "mix_norm": gain((DEPTH, D_MODEL)),
        "ffn2_norm": gain((DEPTH, D_MODEL)),
        "ffn2_w_gate": nrm((DEPTH, D_MODEL, D_FF), D_MODEL ** -0.5),
        "ffn2_w_up": nrm((DEPTH, D_MODEL, D_FF), D_MODEL ** -0.5),
        "ffn2_w_down": nrm((DEPTH, D_FF, D_MODEL), D_FF ** -0.5),
        "ple_norm": gain((DEPTH, D_MODEL)),
        "ple_w_proj": nrm((DEPTH, PLE_DIM, D_MODEL), PLE_DIM ** -0.5),
        "ple_w_gate": nrm((DEPTH, D_MODEL, D_MODEL), D_MODEL ** -0.5),
        "hyb_w_in": nrm((N_EVEN, D_MODEL, HYB_IN), D_MODEL ** -0.5),
        "hyb_w_out": nrm((N_EVEN, HYB_MIX, D_MODEL), HYB_MIX ** -0.5),
        "gla_w_fup": nrm((N_EVEN, GLA_GATE_RANK, GLA_KW), GLA_GATE_RANK ** -0.5),
        "gla_b_f": nrm((N_EVEN, GLA_KW), 0.1),
        "gla_norm": gain((N_EVEN, GLA_VW)),
        "lru_conv_w": nrm((N_EVEN, LRU_CONV_W, LRU_WIDTH), LRU_CONV_W ** -0.5),
        "lru_conv_b": nrm((N_EVEN, LRU_WIDTH), 0.02),
        "lru_w_a": nrm((N_EVEN, LRU_BLOCKS, LRU_BLOCK_W, LRU_BLOCK_W), LRU_BLOCK_W ** -0.5),
        "lru_b_a": nrm((N_EVEN, LRU_WIDTH), 0.02),
        "lru_w_x": nrm((N_EVEN, LRU_BLOCKS, LRU_BLOCK_W, LRU_BLOCK_W), LRU_BLOCK_W ** -0.5),
        "lru_b_x": nrm((N_EVEN, LRU_WIDTH), 0.02),
        "lru_lambda": jnp.log(a_init) - jnp.log1p(-a_init),
        "swa_w_qkv": nrm((N_ODD, D_MODEL, SWA_QKV), D_MODEL ** -0.5),
        "swa_b_qkv": nrm((N_ODD, SWA_QKV), 0.02),
        "swa_w_o": nrm((N_ODD, SWA_HEADS * SWA_HEAD_DIM, D_MODEL), (SWA_HEADS * SWA_HEAD_DIM) ** -0.5),
        "swa_b_o": nrm((N_ODD, D_MODEL), 0.02),
        "swa_sinks": nrm((N_ODD, SWA_HEADS), 0.5),
    }


def reference(x, p, rel_bias, final_norm,
              ffn1_norm, ffn1_w_gate, ffn1_w_up, ffn1_w_down,
              mix_norm,
              ffn2_norm, ffn2_w_gate, ffn2_w_up, ffn2_w_down,
              ple_norm, ple_w_proj, ple_w_gate,
              hyb_w_in, hyb_w_out, gla_w_fup, gla_b_f, gla_norm,
              lru_conv_w, lru_conv_b, lru_w_a, lru_b_a, lru_w_x, lru_b_x, lru_lambda,
              swa_w_qkv, swa_b_qkv, swa_w_o, swa_b_o, swa_sinks):
    for i in range(DEPTH):
        x = x + 0.5 * swiglu(rms_norm(x, ffn1_norm[i]), ffn1_w_gate[i], ffn1_w_up[i], ffn1_w_down[i])
        h = rms_norm(x, mix_norm[i])
        if i % 2 == 0:
            e = i // 2
            x = x + hybrid_mixer(h, hyb_w_in[e], hyb_w_out[e], gla_w_fup[e], gla_b_f[e], gla_norm[e],
                                 lru_conv_w[e], lru_conv_b[e], lru_w_a[e], lru_b_a[e],
                                 lru_w_x[e], lru_b_x[e], lru_lambda[e])
        else:
            o = i // 2
            x = x + swa(h, swa_w_qkv[o], swa_b_qkv[o], swa_w_o[o], swa_b_o[o], swa_sinks[o], rel_bias)
        x = x + 0.5 * swiglu(rms_norm(x, ffn2_norm[i]), ffn2_w_gate[i], ffn2_w_up[i], ffn2_w_down[i])
        gate = jax.nn.sigmoid(rms_norm(x, ple_norm[i]) @ ple_w_gate[i])
        x = x + gate * (p[i] @ ple_w_proj[i])
    return rms_norm(x, final_norm)
```

```python
import contextlib
import numpy as np
import concourse.bass as bass
import concourse.mybir as mybir
from concourse.bass_utils import run_bass_kernel_spmd

F32 = mybir.dt.float32
BF16 = mybir.dt.bfloat16
AF = mybir.ActivationFunctionType
ALU = mybir.AluOpType
AX = mybir.AxisListType

D = 1024
KC = 8
DFF = 2816
NG = 11
DEPTH = 4
PLE = 256
HYB_IN = 2576
EPS = 1e-6
NEG = -30000.0

SEG = 30000
NDSEM = 24
NDSEM_Q = {"pool": 8}


class Op:
    __slots__ = ("eng", "fn", "deps", "sig", "idx", "dma", "dsem", "dval")

    def __init__(self, eng, fn, dma):
        self.eng = eng
        self.fn = fn
        self.dma = dma
        self.deps = ()
        self.sig = False
        self.idx = -1
        self.dsem = -1
        self.dval = 0


class Prog:
    ENGS = ("pe", "act", "dve", "pool", "sp")

    def __init__(self, same_eng_sync=True):
        self.q = {e: [] for e in self.ENGS}
        self.lastw = {}
        self.readers = {}
        self.same = same_eng_sync
        self.ndma = {e: 0 for e in self.ENGS}
        self.extra = []

    def add(self, eng, fn, r=(), w=(), dma=False, noextra=False):
        if self.extra and not noextra:
            r = list(r) + self.extra
        op = Op(eng, fn, dma)
        deps = {}
        for k in r:
            p = self.lastw.get(k)
            if p is not None:
                deps[id(p)] = p
        for k in w:
            p = self.lastw.get(k)
            if p is not None:
                deps[id(p)] = p
            rd = self.readers.get(k)
            if rd:
                for p in rd.values():
                    deps[id(p)] = p
        out = []
        for p in deps.values():
            if p is op:
                continue
            if not p.dma and not dma and p.eng == eng:
                if eng == "pe" or not self.same:
                    continue
            if not p.dma:
                p.sig = True
            out.append(p)
        op.deps = out
        for k in w:
            self.lastw[k] = op
            self.readers[k] = {}
        for k in r:
            d = self.readers.get(k)
            if d is None:
                d = self.readers[k] = {}
            if dma:
                d[("dma", id(op))] = op
            else:
                d[eng] = op
        if dma:
            n = self.ndma[eng]
            self.ndma[eng] = n + 1
            nd = NDSEM_Q.get(eng, NDSEM)
            op.dsem = n % nd
            op.dval = 16 * (n // nd + 1)
        self.q[eng].append(op)
        return op

    def emit(self, nc):
        nsig = {}
        for e in self.ENGS:
            c = 0
            for op in self.q[e]:
                if op.sig and not op.dma:
                    op.idx = c
                    c += 1
            nsig[e] = c
        with contextlib.ExitStack() as st:
            esem = {}
            for e in self.ENGS:
                esem[e] = [st.enter_context(nc.semaphore(f"s_{e}_{i}"))
                           for i in range(nsig[e] // SEG + 1)]
            dsem = {}
            for e in self.ENGS:
                if self.ndma[e]:
                    dsem[e] = [st.enter_context(nc.semaphore(f"d_{e}_{i}"))
                               for i in range(min(NDSEM_Q.get(e, NDSEM), self.ndma[e]))]
            block = st.enter_context(nc.Block())

            def run(ename, eng):
                seen_e = {e: -1 for e in self.ENGS}
                seen_d = {}
                for op in self.q[ename]:
                    for p in op.deps:
                        if p.dma:
                            key = (p.eng, p.dsem)
                            if seen_d.get(key, 0) >= p.dval:
                                continue
                            eng.wait_ge(dsem[p.eng][p.dsem], p.dval)
                            seen_d[key] = p.dval
                        else:
                            if seen_e[p.eng] >= p.idx:
                                continue
                            eng.wait_ge(esem[p.eng][p.idx // SEG], p.idx % SEG + 1)
                            seen_e[p.eng] = p.idx
                    if op.dma:
                        key = (ename, op.dsem)
                        if op.dval > 16 and seen_d.get(key, 0) < op.dval - 16:
                            eng.wait_ge(dsem[ename][op.dsem], op.dval - 16)
                            seen_d[key] = op.dval - 16
                        ins = op.fn(eng)
                        ins.then_inc(dsem[ename][op.dsem], 16)
                    elif op.fn is not None:
                        ins = op.fn(eng)
                        if op.sig:
                            ins.then_inc(esem[ename][op.idx // SEG], 1)
                n = self.ndma[ename]
                nd = NDSEM_Q.get(ename, NDSEM)
                for i in range(min(nd, n)):
                    final = 16 * ((n - 1 - i) // nd + 1)
                    if seen_d.get((ename, i), 0) < final:
                        eng.wait_ge(dsem[ename][i], final)

            @block.tensor
            def _(eng):
                run("pe", eng)

            @block.scalar
            def _(eng):
                run("act", eng)

            @block.vector
            def _(eng):
                run("dve", eng)

            @block.gpsimd
            def _(eng):
                run("pool", eng)

            @block.sync
            def _(eng):
                run("sp", eng)


def _vec_map():
    m = {}
    c = 0

    def put(name, n):
        nonlocal c
        m[name] = c
        c += n

    for l in range(DEPTH):
        for nm in ("ffn1_norm", "mix_norm", "ffn2_norm", "ple_norm"):
            put((nm, l), 8)
    put("final_norm", 8)
    for e in range(2):
        put(("gla_norm", e), 4)
        put(("conv_w", e), 16)
        put(("conv_b", e), 4)
        put(("lru_b_a", e), 4)
        put(("lru_b_x", e), 4)
        put(("lru_lambda", e), 4)
    for o in range(2):
        put(("bq", o), 8)
        put(("bk", o), 4)
        put(("bo", o), 8)
        put(("sinks", o), 16)
    m["_n"] = c
    return m


VM = _vec_map()


def _t5_bucket_np(dist):
    max_exact = 16
    d = np.maximum(dist, 1).astype(np.float32)
    large = max_exact + (np.log(d / max_exact) / np.float32(np.log(128 / max_exact)) * (32 - max_exact)).astype(np.int32)
    large = np.minimum(large, 31)
    return np.where(dist < max_exact, dist, large)


def _colvec(v):
    return np.ascontiguousarray(np.asarray(v, np.float32).reshape(-1, 128).T)


def host_layout(inp):
    vecs = np.zeros((128, VM["_n"]), np.float32)

    def put(key, arr):
        c = VM[key]
        vecs[:, c:c + arr.shape[1]] = arr

    for l in range(DEPTH):
        for nm in ("ffn1_norm", "mix_norm", "ffn2_norm", "ple_norm"):
            put((nm, l), _colvec(inp[nm][l]))
    put("final_norm", _colvec(inp["final_norm"]))
    for e in range(2):
        put(("gla_norm", e), _colvec(inp["gla_norm"][e]))
        cw = np.asarray(inp["lru_conv_w"][e], np.float32)
        cwt = cw.reshape(4, 4, 128).transpose(2, 1, 0).reshape(128, 16)
        put(("conv_w", e), cwt)
        put(("conv_b", e), _colvec(inp["lru_conv_b"][e]))
        put(("lru_b_a", e), _colvec(inp["lru_b_a"][e]))
        put(("lru_b_x", e), _colvec(inp["lru_b_x"][e]))
        put(("lru_lambda", e), _colvec(inp["lru_lambda"][e]))
    for o in range(2):
        b = np.asarray(inp["swa_b_qkv"][o], np.float32)
        put(("bq", o), _colvec(b[:1024]))
        bk = b[1024:1280].reshape(4, 64).T
        put(("bk", o), np.concatenate([bk, bk], axis=0))
        put(("bo", o), _colvec(inp["swa_b_o"][o]))
        put(("sinks", o), np.broadcast_to(np.asarray(inp["swa_sinks"][o], np.float32)[None, :], (128, 16)))
    rows = np.zeros((1, 4 * 256), np.float32)
    for e in range(2):
        rows[0, e * 256:(e + 1) * 256] = inp["gla_b_f"][e]
    for o in range(2):
        rows[0, (2 + o) * 256:(3 + o) * 256] = np.asarray(inp["swa_b_qkv"][o])[1280:1536]
    q = np.arange(128)[:, None] + 128
    k = np.arange(256)[None, :]
    dist = q - k
    valid = (dist >= 0) & (dist < 128)
    bidx = _t5_bucket_np(np.maximum(dist, 0))
    rb = np.asarray(inp["rel_bias"], np.float32)
    tab = rb[bidx]
    tab = np.where(valid[:, :, None], tab, np.float32(NEG))
    bias_t = np.ascontiguousarray(tab.transpose(0, 2, 1))
    j = np.arange(128)[:, None]
    i = np.arange(128)[None, :]
    same = (j // 64) == (i // 64)
    cst = np.zeros((128, 5, 128), np.float32)
    cst[:, 0, :] = np.eye(128)
    cst[:, 1, :] = np.where(same & (j <= i), -1.0 / 16.0, 0.0)
    cst[:, 2, :] = np.where(same & (j > i), -1.0 / 16.0, 0.0)
    cst[:, 3, :] = np.where(same & (j <= i), 1.0, 0.0)
    cst[:, 4, :] = 1.0
    return vecs, rows, bias_t, cst


def build(S, T, TM, L=DEPTH):
    assert S % T == 0 and T % TM == 0 and TM % 128 == 0 and T % 256 == 0
    NPASS = S // T
    NSUB = T // TM
    NTT = TM // 128
    NCH = TM // 64
    nc = bass.Bass("TRN2", target_bir_lowering=False)

    def din(name, shape):
        return nc.dram_tensor(name, list(shape), F32, kind="ExternalInput").ap()

    x_d = din("x", [S, D])
    p_d = din("p", [DEPTH, S, PLE])
    vecs_d = din("vecs", [128, VM["_n"]])
    rows_d = din("rows", [1, 1024])
    bias_d = din("bias_t", [128, 16, 256])
    cst_d = din("cst", [128, 5, 128])
    bvb_d = din("bvb", [128, 2, 256])
    W = {}
    for nm in ("ffn1", "ffn2"):
        W[nm + "_w_gate"] = din(nm + "_w_gate", [DEPTH, D, DFF])
        W[nm + "_w_up"] = din(nm + "_w_up", [DEPTH, D, DFF])
        W[nm + "_w_down"] = din(nm + "_w_down", [DEPTH, DFF, D])
    W["ple_w_proj"] = din("ple_w_proj", [DEPTH, PLE, D])
    W["ple_w_gate"] = din("ple_w_gate", [DEPTH, D, D])
    W["hyb_w_in"] = din("hyb_w_in", [2, D, HYB_IN])
    W["hyb_w_out"] = din("hyb_w_out", [2, D, D])
    W["gla_w_fup"] = din("gla_w_fup", [2, 16, 256])
    W["lru_w_a"] = din("lru_w_a", [2, 8, 64, 64])
    W["lru_w_x"] = din("lru_w_x", [2, 8, 64, 64])
    W["swa_w_qkv"] = din("swa_w_qkv", [2, D, 1536])
    W["swa_w_o"] = din("swa_w_o", [2, D, D])
    out_d = nc.dram_tensor("out", [S, D], F32, kind="ExternalOutput").ap()

    p = Prog()
    st = contextlib.ExitStack()
    with st:
        def sb(name, shape, dt=F32):
            return st.enter_context(nc.sbuf_tensor("sb_" + name, list(shape), dt))

        ps = [st.enter_context(nc.psum_tensor(f"ps{i}", [128, 512], F32)) for i in range(8)]

        def psb(i):
            return ps[i][:].bitcast(BF16)

        def pk(i):
            return ("ps", i)

        xT = sb("xT", [128, KC, T])
        hT = sb("hT", [128, KC, T], BF16)
        vecs = sb("vecs", [128, VM["_n"]])
        rows = sb("rows", [1, 1024])
        bias_t = sb("bias_t", [128, 16, 256], BF16)
        bvb = sb("bvb", [128, 2, 256])
        cst = sb("cst", [128, 5, 128])
        cst_bf = sb("cst_bf", [128, 5, 128], BF16)
        ident, TriC, UpC, ones = cst[:, 0, :], cst[:, 1, :], cst[:, 2, :], cst[:, 4, :]
        ident_bf, maskA_bf, ones_bf = cst_bf[:, 0, :], cst_bf[:, 3, :], cst_bf[:, 4, :]
        dvec = sb("dvec", [128, 64])
        barc = sb("barc", [128, 2])
        NSLOT = 2
        wg = [sb(f"wg{i}", [128, KC, 256], BF16) for i in range(NSLOT)]
        wu = [sb(f"wu{i}", [128, KC, 256], BF16) for i in range(NSLOT)]
        wd = [sb(f"wd{i}", [128, 2, D], BF16) for i in range(NSLOT)]
        WA = sb("WA", [128, KC * 1552], BF16)
        WB = sb("WB", [128, KC * D], BF16)
        WC = sb("WC", [128, 2048], BF16)
        WAv = WA[:].rearrange("q (c f) -> q c f", c=KC)
        WBv = WB[:].rearrange("q (c f) -> q c f", c=KC)
        WCv8 = WC[:].rearrange("q (c f) -> q c f", c=KC)
        WCv2 = WC[:].rearrange("q (c f) -> q c f", c=2)
        wabd = sb("wabd", [128, 4, 128], BF16)
        wxbd = sb("wxbd", [128, 4, 128], BF16)
        sqb = sb("sqb", [128, KC, 256], BF16)
        lnv = sb("lnv", [128, 256])
        rstd = sb("rstd", [128, 256])
        nset_main = [(sqb[:], lnv[:], rstd[:], "m")]
        mixT = sb("mixT", [128, KC, TM], BF16)
        Sst = [sb(f"Sst{e}", [128, 2, 128]) for e in range(2)]
        hst = [sb(f"hst{e}", [128, 4]) for e in range(2)]
        xrh = [sb(f"xrh{e}", [128, 4, 3]) for e in range(2)]
        kTh = [sb(f"kTh{o}", [128, 4, 128], BF16) for o in range(2)]
        vbh = [sb(f"vbh{o}", [128, 256], BF16) for o in range(2)]

        AR_BYTES = 53504
        arena = sb("arena", [128, AR_BYTES // 4])
        ar_off = [0]

        def ar_reset():
            ar_off[0] = 0

        def ar(shape, dt=F32):
            esz = 4 if dt == F32 else 2
            n = 1
            for s_ in shape[1:]:
                n *= s_
            nb = (n * esz + 31) // 32 * 32
            o4 = ar_off[0] // 4
            ar_off[0] += nb
            assert ar_off[0] <= AR_BYTES, (ar_off[0], AR_BYTES)
            v = arena[0:shape[0], o4:o4 + nb // 4]
            if dt != F32:
                v = v.bitcast(dt)
            v = v[:, 0:n]
            if len(shape) == 3:
                v = v.rearrange("q (a b) -> q a b", a=shape[1])
            elif len(shape) == 4:
                v = v.rearrange("q (a b c) -> q a b c", a=shape[1], b=shape[2])
            return v

        ar_reset()
        xtok = [ar([128, D]) for _ in range(2)]
        ostg = ar([128, KC, 256])
        ar_reset()
        sgb = [ar([128, 2, 256]) for _ in range(2)]
        actb = [ar([128, 2, 256], BF16) for _ in range(2)]
        nset_ffn = nset_main + [(ar([128, KC, 256], BF16), ar([128, 256]), ar([128, 256]), "f%d" % i_) for i_ in range(2)]
        ar_reset()
        ptok = ar([128, T // 128, PLE], BF16)
        pT = ar([128, 2, T], BF16)
        gsb = [ar([128, 256]) for _ in range(2)]
        tmpA = [ar([128, 256]) for _ in range(2)]
        nset_ple = nset_main + [(ar([128, KC, 256], BF16), ar([128, 256]), ar([128, 256]), "p%d" % i_) for i_ in range(2)]
        ar_reset()
        S_all = ar([128, NCH, 2, 128], BF16)
        flT = ar([16, TM])
        wfup = ar([16, 256])
        sp_tok = ar([128, NTT, 256])
        edl = ar([128, 256])
        k2 = ar([128, NTT, 256], BF16)
        v_tok = ar([128, NTT, 512], BF16)
        Eb = ar([128, 2, TM])
        Einv = ar([128, 2, TM])
        qdm = ar([128, 4, TM], BF16)
        kdT = ar([128, 2, TM], BF16)
        silr = ar([128, 4, TM], BF16)
        attm = ar([128, 4, 128], BF16)
        lsets = []
        for _ in range(4):
            lsets.append(dict(xrb=ar([128, TM + 3]), xc=ar([128, TM]), xcb=ar([128, TM], BF16), gel=ar([128, TM], BF16),
                              lt=[ar([128, TM]) for _ in range(4)]))
        ar_reset()
        qT = ar([128, 16, TM], BF16)
        kT = ar([128, 4, 128 + TM], BF16)
        vb = ar([128, NTT + 1, 256], BF16)
        ssets = []
        for _ in range(4):
            ssets.append(dict(s_sb=ar([128, 2, 256]), p_sb=ar([128, 2, 256], BF16), pTs=ar([128, 4, 128], BF16),
                              st4=ar([128, 8, 2])))
        o_sb = ar([128, 16, 64], BF16)
        nset_odd = nset_main + [(ar([128, KC, 256], BF16), ar([128, 256]), ar([128, 256]), "o%d" % i_) for i_ in range(2)]

        if SHOW_SBUF:
            print('sbuf bytes remaining', nc.sbuf_bytes_remaining)
        def mm(out, lhsT, rhs, start, stop, r, w):
            p.add("pe", lambda e: e.matmul(out, lhsT=lhsT, rhs=rhs, start=start, stop=stop), r=r, w=w)

        def tr(out, in_, idn, r, w):
            p.add("pe", lambda e: e.transpose(out, in_, idn), r=r, w=w)

        def act(out, in_, func, r, w, bias=None, scale=None, accum_out=None):
            kw = {}
            if bias is not None:
                kw["bias"] = bias
            if scale is not None:
                kw["scale"] = scale
            if accum_out is not None:
                kw["accum_out"] = accum_out
            p.add("act", lambda e: e.activation(out=out, in_=in_, func=func, **kw), r=r, w=w)

        def tt(out, in0, in1, op, r, w):
            p.add("dve", lambda e: e.tensor_tensor(out=out, in0=in0, in1=in1, op=op), r=r, w=w)

        def stt(out, in0, scalar, in1, op0, op1, r, w):
            p.add("dve", lambda e: e.scalar_tensor_tensor(out=out, in0=in0, scalar=scalar, in1=in1, op0=op0, op1=op1),
                  r=r, w=w)

        def ts(out, in0, s1, s2, op0, op1, r, w):
            p.add("dve", lambda e: e.tensor_scalar(out=out, in0=in0, scalar1=s1, scalar2=s2, op0=op0, op1=op1), r=r, w=w)

        def cp(out, in_, r, w):
            p.add("dve", lambda e: e.tensor_copy(out=out, in_=in_), r=r, w=w)

        def mset(ap, val, w):
            p.add("dve", lambda e: e.memset(ap, val), w=w)

        def dma(eng, out, in_, r, w, noextra=False):
            p.add(eng, lambda e: e.dma_start(out=out, in_=in_), r=r, w=w, dma=True, noextra=noextra)

        def barrier():
            p.add("dve", lambda e: e.memset(barc[:], 0.0), w=["AR"], noextra=True)
            p.extra = ["AR"]

        def interleave(gens):
            gens = list(gens)
            while gens:
                for g_ in list(gens):
                    try:
                        next(g_)
                    except StopIteration:
                        gens.remove(g_)

        def xk(c, t0, n):
            return [("x", c, t) for t in range(t0 // 256, (t0 + n + 255) // 256)]

        def hk(c, t0, n):
            return [("h", c, t) for t in range(t0 // 256, (t0 + n + 255) // 256)]

        def hk_all(t0, n):
            return [k for c in range(KC) for k in hk(c, t0, n)]

        def V(key, n=1, off=0):
            c = VM[key] + off
            return vecs[:, c:c + n]

        dma("sp", vecs[:], vecs_d, [], ["vecs"])
        dma("sp", rows[:], rows_d, [], ["rows"])
        dma("sp", cst[:], cst_d, [], ["cst"])
        dma("pool", bias_t[:], bias_d, [], ["bias_t"])
        dma("sp", bvb[:], bvb_d, [], ["bvb"])
        cp(cst_bf[:], cst[:], ["cst"], ["cst_bf"])
        for e in range(2):
            mset(Sst[e][:], 0.0, [("Sst", e, 0), ("Sst", e, 1)])
            mset(hst[e][:], 0.0, [("hst", e, c) for c in range(4)])
            mset(xrh[e][:], 0.0, [("xrh", e, c) for c in range(4)])
        for o in range(2):
            mset(kTh[o][:], 0.0, [("kTh", o)])
            mset(vbh[o][:], 0.0, [("vbh", o)])
        mset(wabd[:], 0.0, [("wabd", g) for g in range(8)])
        mset(wxbd[:], 0.0, [("wxbd", g) for g in range(8)])
        for e in range(2):
            lam = V(("lru_lambda", e), 4)
            act(dvec[:, 40:44], lam, AF.Exp, ["vecs"], ["dv_t0"], scale=-1.0)
            act(dvec[:, 44:48], dvec[:, 40:44], AF.Ln, ["dv_t0"], ["dv_t1"], bias=1.0)
            ts(dvec[:, e * 8:e * 8 + 4], dvec[:, 44:48], -8.0, None, ALU.mult, ALU.bypass, ["dv_t1"], ["dvec"])
            ts(dvec[:, e * 8 + 4:e * 8 + 8], dvec[:, 44:48], -16.0, None, ALU.mult, ALU.bypass, ["dv_t1"], ["dvec"])
        for o in range(2):
            ts(dvec[:, 16 + o * 8:24 + o * 8], V(("bq", o), 8), 0.125, None, ALU.mult, ALU.bypass, ["vecs"], ["dvec"])

        def rmsnorm(gkey, dst, dst_keys_fn, nsets=None):
            nsets = nsets or nset_main
            for ti, a in enumerate(range(0, T, 256)):
                sqb_, lnv_, rstd_, tag = nsets[ti % len(nsets)]
                bank = 7 - (ti % len(nsets))
                xr_keys = [k for c in range(KC) for k in xk(c, a, 256)]
                act(sqb_, xT[:, :, a:a + 256], AF.Square, xr_keys, [("sqb", tag)])
                for c in range(KC):
                    mm(ps[bank][:, 0:256], ones_bf, sqb_[:, c, :], c == 0, c == KC - 1, [("sqb", tag), "cst_bf"], [pk(bank)])
                act(lnv_, ps[bank][:, 0:256], AF.Ln, [pk(bank)], [("lnv", tag)], bias=EPS, scale=1.0 / D)
                act(rstd_, lnv_, AF.Exp, [("lnv", tag)], [("rstd", tag)], scale=-0.5)
                for c in range(KC):
                    stt(dst[:, c, a:a + 256], xT[:, c, a:a + 256], V(gkey, 1, c), rstd_, ALU.mult, ALU.mult,
                        xk(c, a, 256) + [("rstd", tag), "vecs"], dst_keys_fn(c, a, 256))

        ffn_seq = []
        for _ in range(NPASS):
            for l in range(L):
                for nm in ("ffn1", "ffn2"):
                    for g in range(NG):
                        ffn_seq.append((nm, l, g))
        ffn_issued = [0]

        def ffn_issue_loads(upto):
            while ffn_issued[0] < min(upto, len(ffn_seq)):
                i = ffn_issued[0]
                nm, l, g = ffn_seq[i]
                s = i % NSLOT
                src_g = W[nm + "_w_gate"][l].rearrange("(c q) f -> q c f", q=128)[:, :, g * 256:(g + 1) * 256]
                src_u = W[nm + "_w_up"][l].rearrange("(c q) f -> q c f", q=128)[:, :, g * 256:(g + 1) * 256]
                src_d = W[nm + "_w_down"][l][g * 256:(g + 1) * 256, :].rearrange("(j q) d -> q j d", q=128)
                dma("pool", wg[s][:], src_g, [], [("wg", s)], noextra=True)
                dma("pool", wu[s][:], src_u, [], [("wu", s)], noextra=True)
                dma("pool", wd[s][:], src_d, [], [("wd", s)], noextra=True)
                ffn_issued[0] += 1

        ffn_pos = [0]

        def ffn(nm, l):
            barrier()
            rmsnorm((nm + "_norm", l), hT, hk, nset_ffn)
            stages = [(g, t) for g in range(NG) for t in range(T // 256)]
            base = ffn_pos[0]
            ffn_issue_loads(base + NSLOT)

            def GU(si):
                g, t = stages[si]
                s = (base + g) % NSLOT
                for j in range(2):
                    bank = 2 * (si % 2) + j
                    for k in range(KC):
                        mm(ps[bank][:, 0:256], wg[s][:, k, j * 128:(j + 1) * 128], hT[:, k, t * 256:(t + 1) * 256],
                           k == 0, k == KC - 1, [("wg", s), ("h", k, t)], [pk(bank)])
                    for k in range(KC):
                        mm(ps[bank][:, 256:512], wu[s][:, k, j * 128:(j + 1) * 128], hT[:, k, t * 256:(t + 1) * 256],
                           k == 0, k == KC - 1, [("wu", s), ("h", k, t)], [pk(bank)])

            def ACTV(si):
                b = si % 2
                for j in range(2):
                    bank = 2 * b + j
                    act(sgb[b][:, j, :], ps[bank][:, 0:256], AF.Silu, [pk(bank)], [("sgb", b, j)])
                    tt(actb[b][:, j, :], sgb[b][:, j, :], ps[bank][:, 256:512], ALU.mult,
                       [pk(bank), ("sgb", b, j)], [("actb", b, j)])

            def DN(si):
                g, t = stages[si]
                s = (base + g) % NSLOT
                b = si % 2
                for d in range(KC):
                    bank = 4 + d // 2
                    cs = (d % 2) * 256
                    for j in range(2):
                        mm(ps[bank][:, cs:cs + 256], wd[s][:, j, d * 128:(d + 1) * 128], actb[b][:, j, :],
                           j == 0, j == 1, [("wd", s), ("actb", b, j)], [pk(bank)])
                for d in range(KC):
                    bank = 4 + d // 2
                    cs = (d % 2) * 256
                    stt(xT[:, d, t * 256:(t + 1) * 256], ps[bank][:, cs:cs + 256], 0.5, xT[:, d, t * 256:(t + 1) * 256],
                        ALU.mult, ALU.add, [pk(bank)] + xk(d, t * 256, 256), xk(d, t * 256, 256))
                if t == T // 256 - 1:
                    ffn_issue_loads(base + g + NSLOT + 1)

            GU(0)
            for si in range(len(stages)):
                if si + 1 < len(stages):
                    GU(si + 1)
                ACTV(si)
                DN(si)
            ffn_pos[0] += NG

        def ple_loads(l):
            dma("pool", WBv, W["ple_w_gate"][l].rearrange("(c q) f -> q c f", q=128), [], ["WB"], noextra=True)
            dma("pool", WCv2, W["ple_w_proj"][l].rearrange("(c q) f -> q c f", q=128), [], ["WC"], noextra=True)

        def ple(l, tok0):
            barrier()
            dma("pool", ptok, p_d[l, tok0:tok0 + T, :].rearrange("(i q) c -> q i c", q=128), [], ["ptok"])
            for i in range(T // 128):
                for c in range(2):
                    tr(psb(6)[:, c * 128:(c + 1) * 128], ptok[:, i, c * 128:(c + 1) * 128], ident_bf,
                       ["ptok", "cst_bf"], [pk(6)])
                cp(pT[:, :, i * 128:(i + 1) * 128], psb(6)[:, 0:256].rearrange("q (c t) -> q c t", c=2),
                   [pk(6)], [("pT", i)])
            rmsnorm(("ple_norm", l), hT, hk, nset_ple)
            for a in range(0, T, 256):
                t = a // 256
                for d in range(KC):
                    ba, bb = (d % 2) * 2, (d % 2) * 2 + 1
                    for k in range(KC):
                        mm(ps[ba][:, 0:256], WBv[:, k, d * 128:(d + 1) * 128], hT[:, k, a:a + 256], k == 0, k == KC - 1,
                           ["WB", ("h", k, t)], [pk(ba)])
                    for k in range(2):
                        mm(ps[bb][:, 0:256], WCv2[:, k, d * 128:(d + 1) * 128], pT[:, k, a:a + 256], k == 0, k == 1,
                           ["WC", ("pT", 2 * t), ("pT", 2 * t + 1)], [pk(bb)])
                    gs_, tm_ = gsb[d % 2], tmpA[d % 2]
                    act(gs_, ps[ba][:, 0:256], AF.Sigmoid, [pk(ba)], [("gsb", d % 2)])
                    tt(tm_, gs_, ps[bb][:, 0:256], ALU.mult, [("gsb", d % 2), pk(bb)], [("tmpA", d % 2)])
                    tt(xT[:, d, a:a + 256], xT[:, d, a:a + 256], tm_, ALU.add, [("tmpA", d % 2)] + xk(d, a, 256), xk(d, a, 256))

        wa1_sub = [("WA1", kvh, dup) for kvh in range(4) for dup in range(2)]
        wa1_all = ["WA1"] + wa1_sub

        def even_loads_gla(e):
            win = W["hyb_w_in"][e].rearrange("(c q) f -> q c f", q=128)
            dma("pool", WAv[:, :, 0:1024], win[:, :, 0:1024], [], ["WA0"], noextra=True)
            dma("pool", WAv[:, :, 1024:1552], win[:, :, 1024:1552], [], wa1_all, noextra=True)
            dma("pool", WBv, W["hyb_w_out"][e].rearrange("(c q) f -> q c f", q=128), [], ["WB"], noextra=True)
            for g in range(8):
                c, h2 = g // 2, g % 2
                dma("pool", wabd[h2 * 64:(h2 + 1) * 64, c, h2 * 64:(h2 + 1) * 64], W["lru_w_a"][e, g], [], [("wabd", g)],
                    noextra=True)
                dma("pool", wxbd[h2 * 64:(h2 + 1) * 64, c, h2 * 64:(h2 + 1) * 64], W["lru_w_x"][e, g], [], [("wxbd", g)],
                    noextra=True)

        def even_loads_lru(e):
            win = W["hyb_w_in"][e].rearrange("(c q) f -> q c f", q=128)
            dma("pool", WAv[:, :, 0:1024], win[:, :, 1552:2576], [], ["WA0"], noextra=True)

        def even_mixer(l, tok0):
            e = l // 2
            barrier()
            rmsnorm(("mix_norm", l), hT, hk)
            dma("sp", wfup, W["gla_w_fup"][e], [], ["wfup"])
            S = Sst[e]
            for m in range(NSUB if "gla" in PARTS else 0):
                a0 = m * TM
                hkeys = hk_all(a0, TM)

                def proj(bank, col0, ncol, wkeys):
                    for k in range(KC):
                        mm(ps[bank][0:ncol, 0:TM], WAv[:, k, col0:col0 + ncol], hT[:, k, a0:a0 + TM], k == 0, k == KC - 1,
                           wkeys + hkeys, [pk(bank)])

                for h in range(4):
                    orow = slice((1 - h % 2) * 64, (1 - h % 2) * 64 + 64)
                    mset(qdm[orow, h, :], 0.0, [("qdm", h)])
                proj(0, 1536, 16, wa1_all)
                cp(flT, ps[0][0:16, 0:TM], [pk(0)], ["flT"])
                if GLA_STOP == 1:
                    continue
                for i in range(NTT):
                    tk = a0 + i * 128
                    mm(ps[1][:, 0:256], flT[0:16, i * 128:(i + 1) * 128], wfup[0:16, :], True, False, ["flT", "wfup"], [pk(1)])
                    mm(ps[1][:, 0:256], ones[0:1, :], rows[0:1, e * 256:(e + 1) * 256], False, True, ["cst", "rows"], [pk(1)])
                    act(edl, ps[1][:, 0:256], AF.Exp, [pk(1)], ["edl"], scale=-1.0)
                    act(sp_tok[:, i, :], edl, AF.Ln, ["edl"], [("sp_tok", i)], bias=1.0)
                    for hp in range(2):
                        mm(ps[2 + hp][:, i * 128:(i + 1) * 128], sp_tok[:, i, hp * 128:(hp + 1) * 128], TriC, True, True,
                           [("sp_tok", i), "cst"], [pk(2 + hp)])
                    mm(ps[4][:, 0:256], UpC, sp_tok[:, i, :], True, True, [("sp_tok", i), "cst"], [pk(4)])
                    act(edl, ps[4][:, 0:256], AF.Exp, [pk(4)], ["edl"])
                    for k in range(KC):
                        mm(ps[5][:, 0:256], hT[:, k, tk:tk + 128], WAv[:, k, 256:512], k == 0, k == KC - 1,
                           ["WA0"] + hkeys, [pk(5)])
                    tt(k2[:, i, :], ps[5][:, 0:256], edl, ALU.mult, [pk(5), "edl"], [("k2", i)])
                    for k in range(KC):
                        mm(ps[6][:, 0:512], hT[:, k, tk:tk + 128], WAv[:, k, 512:1024], k == 0, k == KC - 1,
                           ["WA0"] + hkeys, [pk(6)])
                    act(v_tok[:, i, :], ps[6][:, 0:512], AF.Copy, [pk(6)], [("v_tok", i)])
                if GLA_STOP == 2:
                    continue
                for hp in range(2):
                    act(Eb[:, hp, :], ps[2 + hp][:, 0:TM], AF.Exp, [pk(2 + hp)], [("Eb", hp)])
                    act(Einv[:, hp, :], ps[2 + hp][:, 0:TM], AF.Exp, [pk(2 + hp)], [("Einv", hp)], scale=-1.0)
                def gen_proj():
                    for hp in range(2):
                        proj(0, hp * 128, 128, ["WA0"])
                        for h2 in range(2):
                            pr = slice(h2 * 64, (h2 + 1) * 64)
                            stt(qdm[pr, 2 * hp + h2, :], ps[0][pr, 0:TM], 0.125, Eb[pr, hp, :], ALU.mult, ALU.mult,
                                [pk(0), ("Eb", hp)], [("qdm", 2 * hp + h2)])
                        yield
                        proj(1, 256 + hp * 128, 128, ["WA0"])
                        tt(kdT[:, hp, :], ps[1][:, 0:TM], Einv[:, hp, :], ALU.mult, [pk(1), ("Einv", hp)], [("kdT", hp)])
                        yield
                    for c in range(4):
                        proj(c % 2, 1024 + c * 128, 128, wa1_all)
                        act(silr[:, c, :], ps[c % 2][:, 0:TM], AF.Silu, [pk(c % 2)], [("silr", c)])
                        yield

                def gen_rec():
                    for n in range(NCH):
                        i, c2 = n // 2, n % 2
                        rsl = slice(c2 * 64, (c2 + 1) * 64)
                        for hp in range(2):
                            cp(S_all[:, n, hp, :], S[:, hp, :], [("Sst", e, hp)], [("S_all", n, hp)])
                            mm(ps[4 + hp][:, 0:256], k2[rsl, i, hp * 128:(hp + 1) * 128], v_tok[rsl, i, hp * 256:(hp + 1) * 256],
                               True, True, [("k2", i), ("v_tok", i)], [pk(4 + hp)])
                            yield
                            for h2 in range(2):
                                pr = slice(h2 * 64, (h2 + 1) * 64)
                                stt(S[pr, hp, :], S[pr, hp, :], Eb[pr, hp, n * 64 + 63:n * 64 + 64],
                                    ps[4 + hp][pr, h2 * 128:(h2 + 1) * 128], ALU.mult, ALU.add,
                                    [("Sst", e, hp), ("Eb", hp), pk(4 + hp)], [("Sst", e, hp)])
                            yield

                interleave([gen_proj(), gen_rec()])
                for i in range(NTT):
                    tsl = slice(i * 128, (i + 1) * 128)
                    for h in range(4):
                        hp, pr = h // 2, slice((h % 2) * 64, (h % 2) * 64 + 64)
                        mm(ps[6][:, h * 128:(h + 1) * 128], kdT[:, hp, tsl], qdm[:, h, tsl], True, True,
                           [("kdT", hp), ("qdm", h)], [pk(6)])
                    for h in range(4):
                        tt(attm[:, h, :], ps[6][:, h * 128:(h + 1) * 128], maskA_bf, ALU.mult, [pk(6), "cst_bf"], [("attm", h)])
                    for h in range(4):
                        hp, pr = h // 2, slice((h % 2) * 64, (h % 2) * 64 + 64)
                        mm(ps[h][:, tsl], v_tok[:, i, h * 128:(h + 1) * 128], attm[:, h, :], True, False,
                           [("v_tok", i), ("attm", h)], [pk(h)])
                        for c2 in range(2):
                            n = 2 * i + c2
                            csl = slice(i * 128 + c2 * 64, i * 128 + c2 * 64 + 64)
                            mm(ps[h][:, csl], S_all[:, n, hp, :], qdm[:, h, csl], False, c2 == 1,
                               [("S_all", n, hp), ("qdm", h)], [pk(h)])
                if GLA_STOP == 5:
                    continue
                def head_norm(h):
                    sqh, lnh, rsh, t1 = lsets[h]["lt"]
                    K_ = lambda nm: (nm, h)
                    nb = 4 + h
                    act(sqh, ps[h][:, 0:TM], AF.Square, [pk(h)], [K_("lt0")])
                    yield
                    mm(ps[nb][:, 0:TM], ones, sqh, True, True, [K_("lt0"), "cst"], [pk(nb)])
                    yield
                    act(lnh, ps[nb][:, 0:TM], AF.Ln, [pk(nb)], [K_("lt1")], bias=EPS, scale=1.0 / 128)
                    yield
                    act(rsh, lnh, AF.Exp, [K_("lt1")], [K_("lt2")], scale=-0.5)
                    yield
                    tt(t1, ps[h][:, 0:TM], rsh, ALU.mult, [pk(h), K_("lt2")], [K_("lt3")])
                    yield
                    stt(mixT[:, h, :], t1, V(("gla_norm", e), 1, h), silr[:, h, :], ALU.mult, ALU.mult,
                        [K_("lt3"), "vecs", ("silr", h)], [("mixT", h)])
                    yield

                interleave([head_norm(h) for h in range(4)])
                for d in range(KC):
                    bank = 4 + d % 2
                    for k in range(4):
                        mm(ps[bank][:, 0:TM], WBv[:, k, d * 128:(d + 1) * 128], mixT[:, k, :], k == 0, k == 3,
                           ["WB", ("mixT", k)], [pk(bank)])
                    tt(xT[:, d, a0:a0 + TM], xT[:, d, a0:a0 + TM], ps[bank][:, 0:TM], ALU.add,
                       [pk(bank)] + xk(d, a0, TM), xk(d, a0, TM))
            even_loads_lru(e)
            cneg = dvec[:, e * 8:e * 8 + 4]
            cneg2 = dvec[:, e * 8 + 4:e * 8 + 8]
            for m in range(NSUB if "lru" in PARTS else 0):
                a0 = m * TM
                hkeys = hk_all(a0, TM)
                def lru_block(c, si):
                    L_ = lsets[si]
                    xrb, xc, xcb, gel, lt = L_["xrb"], L_["xc"], L_["xcb"], L_["gel"], L_["lt"]
                    K_ = lambda nm: (nm, si)
                    b0, b1 = 2 * si, 2 * si + 1
                    for k in range(KC):
                        mm(ps[b0][:, 0:TM], WAv[:, k, c * 128:(c + 1) * 128], hT[:, k, a0:a0 + TM], k == 0, k == KC - 1,
                           ["WA0"] + hkeys, [pk(b0)])
                    cp(xrb[:, 0:3], xrh[e][:, c, :], [("xrh", e, c)], [K_("xrb")])
                    act(xrb[:, 3:3 + TM], ps[b0][:, 0:TM], AF.Copy, [pk(b0)], [K_("xrb")])
                    yield
                    for k in range(KC):
                        mm(ps[b1][:, 0:TM], WAv[:, k, 512 + c * 128:512 + (c + 1) * 128], hT[:, k, a0:a0 + TM],
                           k == 0, k == KC - 1, ["WA0"] + hkeys, [pk(b1)])
                    act(lt[0], ps[b1][:, 0:TM], AF.Square, [pk(b1)], [K_("lt0")])
                    yield
                    ts(lt[0], lt[0], 0.044715, 1.0, ALU.mult, ALU.add, [K_("lt0")], [K_("lt0")])
                    tt(lt[0], lt[0], ps[b1][:, 0:TM], ALU.mult, [K_("lt0"), pk(b1)], [K_("lt0")])
                    yield
                    act(lt[1], lt[0], AF.Sigmoid, [K_("lt0")], [K_("lt1")], scale=1.5957691216057308)
                    yield
                    tt(gel, lt[1], ps[b1][:, 0:TM], ALU.mult, [K_("lt1"), pk(b1)], [K_("gel")])
                    cw = VM[("conv_w", e)] + c * 4
                    ts(xc, xrb[:, 0:TM], vecs[:, cw:cw + 1], V(("conv_b", e), 1, c), ALU.mult, ALU.add,
                       [K_("xrb"), "vecs"], [K_("xc")])
                    yield
                    for tap in range(1, 4):
                        stt(xc, xrb[:, tap:tap + TM], vecs[:, cw + tap:cw + tap + 1], xc, ALU.mult, ALU.add,
                            [K_("xrb"), K_("xc"), "vecs"], [K_("xc")])
                        yield
                    cp(xrh[e][:, c, :], xrb[:, TM:TM + 3], [K_("xrb")], [("xrh", e, c)])
                    act(xcb, xc, AF.Copy, [K_("xc")], [K_("xcb")])
                    yield
                    mm(ps[b0][:, 0:TM], wabd[:, c, :], xcb, True, True, [("wabd", 2 * c), ("wabd", 2 * c + 1), K_("xcb")], [pk(b0)])
                    mm(ps[b1][:, 0:TM], wxbd[:, c, :], xcb, True, True, [("wxbd", 2 * c), ("wxbd", 2 * c + 1), K_("xcb")], [pk(b1)])
                    act(lt[2], ps[b0][:, 0:TM], AF.Sigmoid, [pk(b0), "vecs"], [K_("lt2")], bias=V(("lru_b_a", e), 1, c))
                    act(lt[3], ps[b1][:, 0:TM], AF.Sigmoid, [pk(b1), "vecs"], [K_("lt3")], bias=V(("lru_b_x", e), 1, c))
                    yield
                    act(lt[0], lt[2], AF.Exp, [K_("lt2"), "dvec"], [K_("lt0")], scale=cneg[:, c:c + 1])
                    act(lt[1], lt[2], AF.Exp, [K_("lt2"), "dvec"], [K_("lt1")], scale=cneg2[:, c:c + 1])
                    tt(lt[3], lt[3], xc, ALU.mult, [K_("lt3"), K_("xc")], [K_("lt3")])
                    yield
                    ts(lt[1], lt[1], -1.0, 1.0, ALU.mult, ALU.add, [K_("lt1")], [K_("lt1")])
                    yield
                    act(lt[1], lt[1], AF.Sqrt, [K_("lt1")], [K_("lt1")])
                    yield
                    tt(lt[3], lt[3], lt[1], ALU.mult, [K_("lt3"), K_("lt1")], [K_("lt3")])
                    yield
                    p.add("dve", lambda eng: eng.tensor_tensor_scan(out=lt[2], data0=lt[0], data1=lt[3],
                                                                    initial=hst[e][:, c:c + 1], op0=ALU.mult, op1=ALU.add),
                          r=[K_("lt0"), K_("lt3"), ("hst", e, c)], w=[K_("lt2")])
                    yield
                    cp(hst[e][:, c:c + 1], lt[2][:, TM - 1:TM], [K_("lt2")], [("hst", e, c)])
                    tt(mixT[:, 4 + c, :], lt[2], gel, ALU.mult, [K_("lt2"), K_("gel")], [("mixT", 4 + c)])
                    yield

                interleave([lru_block(c, c) for c in range(4)])
                for d in range(KC):
                    bank = 4 + d % 2
                    for k in range(4):
                        mm(ps[bank][:, 0:TM], WBv[:, 4 + k, d * 128:(d + 1) * 128], mixT[:, 4 + k, :], k == 0, k == 3,
                           ["WB", ("mixT", 4 + k)], [pk(bank)])
                    tt(xT[:, d, a0:a0 + TM], xT[:, d, a0:a0 + TM], ps[bank][:, 0:TM], ALU.add,
                       [pk(bank)] + xk(d, a0, TM), xk(d, a0, TM))

        def odd_loads(o):
            wqkv = W["swa_w_qkv"][o].rearrange("(c q) f -> q c f", q=128)
            dma("pool", WAv[:, :, 0:1024], wqkv[:, :, 0:1024], [], ["WA0"], noextra=True)
            for kvh in range(4):
                for dup in range(2):
                    c0 = 1024 + kvh * 128 + dup * 64
                    dma("pool", WAv[:, :, c0:c0 + 64], wqkv[:, :, 1024 + kvh * 64:1024 + kvh * 64 + 64], [],
                        [("WA1", kvh, dup)], noextra=True)
            dma("pool", WBv, W["swa_w_o"][o].rearrange("(c q) f -> q c f", q=128), [], ["WB"], noextra=True)
            dma("pool", WCv8, wqkv[:, :, 1280:1536], [], ["WC"], noextra=True)

        def odd_mixer(l, tok0):
            o = l // 2
            barrier()
            rmsnorm(("mix_norm", l), hT, hk, nset_odd)
            for m in range(NSUB):
                a0 = m * TM
                hkeys = hk_all(a0, TM)
                for blk in range(8):
                    bank = 6 + blk % 2
                    for k in range(KC):
                        mm(ps[bank][:, 0:TM], WAv[:, k, blk * 128:(blk + 1) * 128], hT[:, k, a0:a0 + TM], k == 0, k == KC - 1,
                           ["WA0"] + hkeys, [pk(bank)])
                    for h2 in range(2):
                        pr = slice(h2 * 64, (h2 + 1) * 64)
                        orow = slice((1 - h2) * 64, (1 - h2) * 64 + 64)
                        mset(qT[orow, 2 * blk + h2, :], 0.0, [("qT", 2 * blk + h2)])
                        act(qT[pr, 2 * blk + h2, :], ps[bank][pr, 0:TM], AF.Identity, [pk(bank), "dvec"],
                            [("qT", 2 * blk + h2)], bias=dvec[pr, 16 + o * 8 + blk:16 + o * 8 + blk + 1], scale=0.125)
                if ODD_STOP == 1:
                    continue
                cp(kT[:, :, 0:128], kTh[o][:], [("kTh", o)], ["kT"])
                cp(vb[:, 0, :], vbh[o][:], [("vbh", o)], [("vb", 0)])
                for kvh in range(4):
                    bank = 6 + kvh % 2
                    for k in range(KC):
                        mm(ps[bank][:, 0:TM], WAv[:, k, 1024 + kvh * 128:1024 + (kvh + 1) * 128], hT[:, k, a0:a0 + TM],
                           k == 0, k == KC - 1, [("WA1", kvh, 0), ("WA1", kvh, 1)] + hkeys, [pk(bank)])
                    act(kT[:, kvh, 128:128 + TM], ps[bank][:, 0:TM], AF.Identity, [pk(bank), "vecs"], ["kT"],
                        bias=V(("bk", o), 1, kvh))
                for i in range(NTT):
                    tk = a0 + i * 128
                    bank = 6 + i % 2
                    for k in range(KC):
                        mm(ps[bank][:, 0:256], hT[:, k, tk:tk + 128], WCv8[:, k, :], k == 0, k == KC - 1, ["WC"] + hkeys,
                           [pk(bank)])
                    tt(vb[:, 1 + i, :], ps[bank][:, 0:256], bvb[:, o, :], ALU.add, [pk(bank), "bvb"], [("vb", 1 + i)])
                cp(kTh[o][:], kT[:, :, TM:TM + 128], ["kT"], [("kTh", o)])
                cp(vbh[o][:], vb[:, NTT, :], [("vb", NTT)], [("vbh", o)])
                if ODD_STOP == 2:
                    continue
                for i in range(NTT):
                    gb = (tok0 + a0) // 128 + i
                    qsl = slice(i * 128, (i + 1) * 128)
                    def swa_stream(kvh, hb, si):
                        S_ = ssets[si]
                        s_sb, p_sb, pTs, st4 = S_["s_sb"], S_["p_sb"], S_["pTs"], S_["st4"]
                        negmx, rsum, t4, es, den, rden = (st4[:, ii, :] for ii in range(6))
                        K_ = lambda nm: (nm, si)
                        sbk = (0, 1, 6, 7)[si]
                        tb = 2 if si < 2 else 5
                        toff = (si % 2) * 512
                        h0 = 4 * kvh + 2 * hb
                        ob = 3 + h0 // 8
                        for g in range(2):
                            mm(ps[sbk][:, g * 256:g * 256 + 256], qT[:, h0 + g, qsl],
                               kT[:, kvh, i * 128:i * 128 + 256], True, True, [("qT", h0 + g), "kT"], [pk(sbk)])
                        yield
                        tt(s_sb, ps[sbk][:, 0:512].rearrange("q (g t) -> q g t", g=2),
                           bias_t[:, h0:h0 + 2, :], ALU.add, [pk(sbk), "bias_t"], [K_("s_sb")])
                        if gb == 0:
                            mset(s_sb[:, :, 0:128], NEG, [K_("s_sb")])
                        yield
                        p.add("dve", lambda eng: eng.tensor_reduce(out=negmx, in_=s_sb, axis=AX.X, op=ALU.max, negate=True),
                              r=[K_("s_sb")], w=[K_("negmx")])
                        yield
                        for g in range(2):
                            act(p_sb[:, g, :], s_sb[:, g, :], AF.Exp, [K_("s_sb"), K_("negmx")], [("p_sb", si, g), ("rsum", si, g)],
                                bias=negmx[:, g:g + 1], accum_out=rsum[:, g:g + 1])
                        tt(t4, negmx, V(("sinks", o), 2, h0), ALU.add, [K_("negmx"), "vecs"], [K_("t4")])
                        yield
                        act(es, t4, AF.Exp, [K_("t4")], [K_("es")])
                        for g in range(2):
                            for half in range(2):
                                tr(psb(tb)[:, toff + (g * 2 + half) * 128:toff + (g * 2 + half + 1) * 128],
                                   p_sb[:, g, half * 128:(half + 1) * 128], ident_bf, [("p_sb", si, g), "cst_bf"], [pk(tb)])
                        yield
                        tt(den, rsum, es, ALU.add, [("rsum", si, g) for g in range(2)] + [K_("es")], [K_("den")])
                        cp(pTs, psb(tb)[:, toff:toff + 512].rearrange("q (a t) -> q a t", a=4), [pk(tb)], [K_("pTs")])
                        yield
                        p.add("dve", lambda eng: eng.reciprocal(out=rden, in_=den), r=[K_("den")], w=[K_("rden")])
                        for g in range(2):
                            h = h0 + g
                            for half in range(2):
                                mm(ps[ob][:, (h % 8) * 64:(h % 8) * 64 + 64], pTs[:, g * 2 + half, :],
                                   vb[:, i + half, kvh * 64:(kvh + 1) * 64], half == 0, half == 1,
                                   [K_("pTs"), ("vb", i + half)], [pk(ob)])
                        yield
                        tt(o_sb[:, h0:h0 + 2, :],
                           ps[ob][:, (h0 % 8) * 64:(h0 % 8) * 64 + 128].rearrange("q (g d) -> q g d", g=2),
                           rden.unsqueeze(2).to_broadcast([128, 2, 64]), ALU.mult, [pk(ob), K_("rden")],
                           [("o_sb", h0 // 2)])
                        yield

                    for kq in ((0, 2), (1, 3)):
                        interleave([swa_stream(kq[0], 0, 0), swa_stream(kq[1], 0, 2),
                                    swa_stream(kq[0], 1, 1), swa_stream(kq[1], 1, 3)])
                    if ODD_STOP == 7:
                        continue
                    for b2 in range(8):
                        tr(psb(2)[:, b2 * 128:(b2 + 1) * 128], o_sb[:, 2 * b2:2 * b2 + 2, :].rearrange("q a d -> q (a d)"),
                           ident_bf, [("o_sb", b2), "cst_bf"], [pk(2)])
                    cp(mixT[:, :, qsl], psb(2)[:, 0:1024].rearrange("q (a t) -> q a t", a=8), [pk(2)],
                       [("mixT", k) for k in range(8)])
                for d in range(KC):
                    bank = 6 + d % 2
                    for k in range(KC):
                        mm(ps[bank][:, 0:TM], WBv[:, k, d * 128:(d + 1) * 128], mixT[:, k, :], k == 0, k == KC - 1,
                           ["WB", ("mixT", k)], [pk(bank)])
                    stt(xT[:, d, a0:a0 + TM], ps[bank][:, 0:TM], V(("bo", o), 1, d), xT[:, d, a0:a0 + TM], ALU.add, ALU.add,
                        [pk(bank), "vecs"] + xk(d, a0, TM), xk(d, a0, TM))

        for ps_i in range(NPASS):
            tok0 = ps_i * T
            barrier()
            for i in range(T // 128):
                xb_ = xtok[i % 2]
                dma("sp", xb_, x_d[tok0 + i * 128:tok0 + (i + 1) * 128, :], [], [("xtok", i % 2)])
                for half in range(2):
                    for c in range(4):
                        tr(ps[half][:, c * 128:(c + 1) * 128], xb_[:, (half * 4 + c) * 128:(half * 4 + c + 1) * 128], ident,
                           [("xtok", i % 2), "cst"], [pk(half)])
                    cp(xT[:, half * 4:half * 4 + 4, i * 128:(i + 1) * 128],
                       ps[half][:, 0:512].rearrange("q (c t) -> q c t", c=4), [pk(half)],
                       [k for c in range(half * 4, half * 4 + 4) for k in xk(c, i * 128, 128)])
            for l in range(L):
                if "mix" in PARTS:
                    if l % 2 == 0:
                        even_loads_gla(l // 2)
                    else:
                        odd_loads(l // 2)
                if "ffn1" in PARTS:
                    ffn("ffn1", l)
                else:
                    ffn_pos[0] += NG
                    ffn_issued[0] = max(ffn_issued[0], ffn_pos[0])
                if "mix" in PARTS:
                    if l % 2 == 0:
                        even_mixer(l, tok0)
                    else:
                        odd_mixer(l, tok0)
                if "ple" in PARTS:
                    ple_loads(l)
                if "ffn2" in PARTS:
                    ffn("ffn2", l)
                else:
                    ffn_pos[0] += NG
                    ffn_issued[0] = max(ffn_issued[0], ffn_pos[0])
                if "ple" in PARTS:
                    ple(l, tok0)
            barrier()
            for a in range(0, T, 256):
                xr_keys = [k for c in range(KC) for k in xk(c, a, 256)]
                act(ostg, xT[:, :, a:a + 256], AF.Square, xr_keys, ["sq"])
                for c in range(KC):
                    mm(ps[7][:, 0:256], ones, ostg[:, c, :], c == 0, c == KC - 1, ["sq", "cst"], [pk(7)])
                act(lnv[:], ps[7][:, 0:256], AF.Ln, [pk(7)], ["lnv"], bias=EPS, scale=1.0 / D)
                act(rstd[:], lnv[:], AF.Exp, ["lnv"], ["rstd"], scale=-0.5)
                for c in range(KC):
                    stt(ostg[:, c, :], xT[:, c, a:a + 256], V("final_norm", 1, c), rstd[:], ALU.mult, ALU.mult,
                        xk(c, a, 256) + ["rstd", "vecs", "sq"], ["sq"])
                for i in range(2):
                    ob_ = xtok[i % 2]
                    for half in range(2):
                        for c in range(4):
                            tr(ps[half][:, c * 128:(c + 1) * 128], ostg[:, half * 4 + c, i * 128:(i + 1) * 128], ident,
                               ["sq", "cst"], [pk(half)])
                        cp(ob_[:, half * 512:(half + 1) * 512], ps[half][:, 0:512], [pk(half)], [("xtok", i % 2)])
                    r0 = tok0 + a + i * 128
                    dma("sp", out_d[r0:r0 + 128, :], ob_, [("xtok", i % 2)], [("out", r0)])
        p.add("sp", None, r=[("out", r0) for r0 in range(0, S, 128)])
        p.emit(nc)
    return nc


SHOW_SBUF = False
GLA_STOP = 0
ODD_STOP = 0
PARTS = {"ffn1", "mix", "ffn2", "ple", "gla", "lru"}
S_FULL = 4096
T_PASS = 1024
T_MIX = 256
N_CORES = 4

_W_NAMES = ["ffn1_w_gate", "ffn1_w_up", "ffn1_w_down", "ffn2_w_gate", "ffn2_w_up", "ffn2_w_down",
            "ple_w_proj", "ple_w_gate", "hyb_w_in", "hyb_w_out", "gla_w_fup", "lru_w_a", "lru_w_x",
            "swa_w_qkv", "swa_w_o"]


def make_in_maps(inp, n_seq, S):
    vecs, rows, bias_t, cst = host_layout(inp)
    bvb = np.ascontiguousarray(np.broadcast_to(rows[0, 512:1024].reshape(1, 2, 256), (128, 2, 256)))
    shared = {"vecs": vecs, "rows": rows, "bias_t": bias_t, "cst": cst, "bvb": bvb}
    for nm in _W_NAMES:
        shared[nm] = np.ascontiguousarray(np.asarray(inp[nm], np.float32))
    maps = []
    for b in range(n_seq):
        m = dict(shared)
        m["x"] = np.ascontiguousarray(np.asarray(inp["x"][b, :S], np.float32))
        m["p"] = np.ascontiguousarray(np.asarray(inp["p"][:, b, :S], np.float32))
        maps.append(m)
    return maps


def kernel(**inputs):
    nc = build(S_FULL, T_PASS, T_MIX)
    maps = make_in_maps(inputs, N_CORES, S_FULL)
    res = run_bass_kernel_spmd(nc, maps, core_ids=list(range(N_CORES)))
    return np.stack([np.asarray(r["out"], np.float32) for r in res.results], axis=0)
```

```python
import contextlib
import numpy as np
import concourse.bass as bass
import concourse.mybir as mybir
from concourse.bass_utils import run_bass_kernel_spmd

F32 = mybir.dt.float32
BF16 = mybir.dt.bfloat16
AF = mybir.ActivationFunctionType
ALU = mybir.AluOpType
AX = mybir.AxisListType

D = 1024
KC = 8
DFF = 2816
NG = 11
DEPTH = 4
PLE = 256
HYB_IN = 2576
EPS = 1e-6
NEG = -30000.0

SEG = 30000
NDSEM = 24
NDSEM_Q = {"pool": 8}


class Op:
    __slots__ = ("eng", "fn", "deps", "sig", "idx", "dma", "dsem", "dval")

    def __init__(self, eng, fn, dma):
        self.eng = eng
        self.fn = fn
        self.dma = dma
        self.deps = ()
        self.sig = False
        self.idx = -1
        self.dsem = -1
        self.dval = 0


class Prog:
    ENGS = ("pe", "act", "dve", "pool", "sp")

    def __init__(self, same_eng_sync=True):
        self.q = {e: [] for e in self.ENGS}
        self.lastw = {}
        self.readers = {}
        self.same = same_eng_sync
        self.ndma = {e: 0 for e in self.ENGS}
        self.extra = []

    def add(self, eng, fn, r=(), w=(), dma=False, noextra=False):
        if self.extra and not noextra:
            r = list(r) + self.extra
        op = Op(eng, fn, dma)
        deps = {}
        for k in r:
            p = self.lastw.get(k)
            if p is not None:
                deps[id(p)] = p
        for k in w:
            p = self.lastw.get(k)
            if p is not None:
                deps[id(p)] = p
            rd = self.readers.get(k)
            if rd:
                for p in rd.values():
                    deps[id(p)] = p
        out = []
        for p in deps.values():
            if p is op:
                continue
            if not p.dma and not dma and p.eng == eng:
                if eng == "pe" or not self.same:
                    continue
            if not p.dma:
                p.sig = True
            out.append(p)
        op.deps = out
        for k in w:
            self.lastw[k] = op
            self.readers[k] = {}
        for k in r:
            d = self.readers.get(k)
            if d is None:
                d = self.readers[k] = {}
            if dma:
                d[("dma", id(op))] = op
            else:
                d[eng] = op
        if dma:
            n = self.ndma[eng]
            self.ndma[eng] = n + 1
            nd = NDSEM_Q.get(eng, NDSEM)
            op.dsem = n % nd
            op.dval = 16 * (n // nd + 1)
        self.q[eng].append(op)
        return op

    def emit(self, nc):
        nsig = {}
        for e in self.ENGS:
            c = 0
            for op in self.q[e]:
                if op.sig and not op.dma:
                    op.idx = c
                    c += 1
            nsig[e] = c
        with contextlib.ExitStack() as st:
            esem = {}
            for e in self.ENGS:
                esem[e] = [st.enter_context(nc.semaphore(f"s_{e}_{i}"))
                           for i in range(nsig[e] // SEG + 1)]
            dsem = {}
            for e in self.ENGS:
                if self.ndma[e]:
                    dsem[e] = [st.enter_context(nc.semaphore(f"d_{e}_{i}"))
                               for i in range(min(NDSEM_Q.get(e, NDSEM), self.ndma[e]))]
            block = st.enter_context(nc.Block())

            def run(ename, eng):
                seen_e = {e: -1 for e in self.ENGS}
                seen_d = {}
                for op in self.q[ename]:
                    for p in op.deps:
                        if p.dma:
                            key = (p.eng, p.dsem)
                            if seen_d.get(key, 0) >= p.dval:
                                continue
                            eng.wait_ge(dsem[p.eng][p.dsem], p.dval)
                            seen_d[key] = p.dval
                        else:
                            if seen_e[p.eng] >= p.idx:
                                continue
                            eng.wait_ge(esem[p.eng][p.idx // SEG], p.idx % SEG + 1)
                            seen_e[p.eng] = p.idx
                    if op.dma:
                        key = (ename, op.dsem)
                        if op.dval > 16 and seen_d.get(key, 0) < op.dval - 16:
                            eng.wait_ge(dsem[ename][op.dsem], op.dval - 16)
                            seen_d[key] = op.dval - 16
                        ins = op.fn(eng)
                        ins.then_inc(dsem[ename][op.dsem], 16)
                    elif op.fn is not None:
                        ins = op.fn(eng)
                        if op.sig:
                            ins.then_inc(esem[ename][op.idx // SEG], 1)
                n = self.ndma[ename]
                nd = NDSEM_Q.get(ename, NDSEM)
                for i in range(min(nd, n)):
                    final = 16 * ((n - 1 - i) // nd + 1)
                    if seen_d.get((ename, i), 0) < final:
                        eng.wait_ge(dsem[ename][i], final)

            @block.tensor
            def _(eng):
                run("pe", eng)

            @block.scalar
            def _(eng):
                run("act", eng)

            @block.vector
            def _(eng):
                run("dve", eng)

            @block.gpsimd
            def _(eng):
                run("pool", eng)

            @block.sync
            def _(eng):
                run("sp", eng)


def _vec_map():
    m = {}
    c = 0

    def put(name, n):
        nonlocal c
        m[name] = c
        c += n

    for l in range(DEPTH):
        for nm in ("ffn1_norm", "mix_norm", "ffn2_norm", "ple_norm"):
            put((nm, l), 8)
    put("final_norm", 8)
    for e in range(2):
        put(("gla_norm", e), 4)
        put(("conv_w", e), 16)
        put(("conv_b", e), 4)
        put(("lru_b_a", e), 4)
        put(("lru_b_x", e), 4)
        put(("lru_lambda", e), 4)
    for o in range(2):
        put(("bq", o), 8)
        put(("bk", o), 4)
        put(("bo", o), 8)
        put(("sinks", o), 16)
    m["_n"] = c
    return m


VM = _vec_map()


def _t5_bucket_np(dist):
    max_exact = 16
    d = np.maximum(dist, 1).astype(np.float32)
    large = max_exact + (np.log(d / max_exact) / np.float32(np.log(128 / max_exact)) * (32 - max_exact)).astype(np.int32)
    large = np.minimum(large, 31)
    return np.where(dist < max_exact, dist, large)


def _colvec(v):
    return np.ascontiguousarray(np.asarray(v, np.float32).reshape(-1, 128).T)


def host_layout(inp):
    vecs = np.zeros((128, VM["_n"]), np.float32)

    def put(key, arr):
        c = VM[key]
        vecs[:, c:c + arr.shape[1]] = arr

    for l in range(DEPTH):
        for nm in ("ffn1_norm", "mix_norm", "ffn2_norm", "ple_norm"):
            put((nm, l), _colvec(inp[nm][l]))
    put("final_norm", _colvec(inp["final_norm"]))
    for e in range(2):
        put(("gla_norm", e), _colvec(inp["gla_norm"][e]))
        cw = np.asarray(inp["lru_conv_w"][e], np.float32)
        cwt = cw.reshape(4, 4, 128).transpose(2, 1, 0).reshape(128, 16)
        put(("conv_w", e), cwt)
        put(("conv_b", e), _colvec(inp["lru_conv_b"][e]))
        put(("lru_b_a", e), _colvec(inp["lru_b_a"][e]))
        put(("lru_b_x", e), _colvec(inp["lru_b_x"][e]))
        put(("lru_lambda", e), _colvec(inp["lru_lambda"][e]))
    for o in range(2):
        b = np.asarray(inp["swa_b_qkv"][o], np.float32)
        put(("bq", o), _colvec(b[:1024]))
        bk = b[1024:1280].reshape(4, 64).T
        put(("bk", o), np.concatenate([bk, bk], axis=0))
        put(("bo", o), _colvec(inp["swa_b_o"][o]))
        put(("sinks", o), np.broadcast_to(np.asarray(inp["swa_sinks"][o], np.float32)[None, :], (128, 16)))
    rows = np.zeros((1, 4 * 256), np.float32)
    for e in range(2):
        rows[0, e * 256:(e + 1) * 256] = inp["gla_b_f"][e]
    for o in range(2):
        rows[0, (2 + o) * 256:(3 + o) * 256] = np.asarray(inp["swa_b_qkv"][o])[1280:1536]
    q = np.arange(128)[:, None] + 128
    k = np.arange(256)[None, :]
    dist = q - k
    valid = (dist >= 0) & (dist < 128)
    bidx = _t5_bucket_np(np.maximum(dist, 0))
    rb = np.asarray(inp["rel_bias"], np.float32)
    tab = rb[bidx]
    tab = np.where(valid[:, :, None], tab, np.float32(NEG))
    bias_t = np.ascontiguousarray(tab.transpose(0, 2, 1))
    j = np.arange(128)[:, None]
    i = np.arange(128)[None, :]
    same = (j // 64) == (i // 64)
    cst = np.zeros((128, 5, 128), np.float32)
    cst[:, 0, :] = np.eye(128)
    cst[:, 1, :] = np.where(same & (j <= i), -1.0 / 16.0, 0.0)
    cst[:, 2, :] = np.where(same & (j > i), -1.0 / 16.0, 0.0)
    cst[:, 3, :] = np.where(same & (j <= i), 1.0, 0.0)
    cst[:, 4, :] = 1.0
    return vecs, rows, bias_t, cst


def build(S, T, TM, L=DEPTH):
    assert S % T == 0 and T % TM == 0 and TM % 128 == 0 and T % 256 == 0
    NPASS = S // T
    NSUB = T // TM
    NTT = TM // 128
    NCH = TM // 64
    nc = bass.Bass("TRN2", target_bir_lowering=False)

    def din(name, shape):
        return nc.dram_tensor(name, list(shape), F32, kind="ExternalInput").ap()

    x_d = din("x", [S, D])
    p_d = din("p", [DEPTH, S, PLE])
    vecs_d = din("vecs", [128, VM["_n"]])
    rows_d = din("rows", [1, 1024])
    bias_d = din("bias_t", [128, 16, 256])
    cst_d = din("cst", [128, 5, 128])
    bvb_d = din("bvb", [128, 2, 256])
    W = {}
    for nm in ("ffn1", "ffn2"):
        W[nm + "_w_gate"] = din(nm + "_w_gate", [DEPTH, D, DFF])
        W[nm + "_w_up"] = din(nm + "_w_up", [DEPTH, D, DFF])
        W[nm + "_w_down"] = din(nm + "_w_down", [DEPTH, DFF, D])
    W["ple_w_proj"] = din("ple_w_proj", [DEPTH, PLE, D])
    W["ple_w_gate"] = din("ple_w_gate", [DEPTH, D, D])
    W["hyb_w_in"] = din("hyb_w_in", [2, D, HYB_IN])
    W["hyb_w_out"] = din("hyb_w_out", [2, D, D])
    W["gla_w_fup"] = din("gla_w_fup", [2, 16, 256])
    W["lru_w_a"] = din("lru_w_a", [2, 8, 64, 64])
    W["lru_w_x"] = din("lru_w_x", [2, 8, 64, 64])
    W["swa_w_qkv"] = din("swa_w_qkv", [2, D, 1536])
    W["swa_w_o"] = din("swa_w_o", [2, D, D])
    out_d = nc.dram_tensor("out", [S, D], F32, kind="ExternalOutput").ap()

    p = Prog()
    st = contextlib.ExitStack()
    with st:
        def sb(name, shape, dt=F32):
            return st.enter_context(nc.sbuf_tensor("sb_" + name, list(shape), dt))

        ps = [st.enter_context(nc.psum_tensor(f"ps{i}", [128, 512], F32)) for i in range(8)]

        def psb(i):
            return ps[i][:].bitcast(BF16)

        def pk(i):
            return ("ps", i)

        xT = sb("xT", [128, KC, T])
        hT = sb("hT", [128, KC, T], BF16)
        vecs = sb("vecs", [128, VM["_n"]])
        rows = sb("rows", [1, 1024])
        bias_t = sb("bias_t", [128, 16, 256], BF16)
        bvb = sb("bvb", [128, 2, 256])
        cst = sb("cst", [128, 5, 128])
        cst_bf = sb("cst_bf", [128, 5, 128], BF16)
        ident, TriC, UpC, ones = cst[:, 0, :], cst[:, 1, :], cst[:, 2, :], cst[:, 4, :]
        ident_bf, maskA_bf, ones_bf = cst_bf[:, 0, :], cst_bf[:, 3, :], cst_bf[:, 4, :]
        dvec = sb("dvec", [128, 64])
        barc = sb("barc", [128, 2])
        NSLOT = 2
        wg = [sb(f"wg{i}", [128, KC, 256], BF16) for i in range(NSLOT)]
        wu = [sb(f"wu{i}", [128, KC, 256], BF16) for i in range(NSLOT)]
        wd = [sb(f"wd{i}", [128, 2, D], BF16) for i in range(NSLOT)]
        WA = sb("WA", [128, KC * 1552], BF16)
        WB = sb("WB", [128, KC * D], BF16)
        WC = sb("WC", [128, 2048], BF16)
        WAv = WA[:].rearrange("q (c f) -> q c f", c=KC)
        WBv = WB[:].rearrange("q (c f) -> q c f", c=KC)
        WCv8 = WC[:].rearrange("q (c f) -> q c f", c=KC)
        WCv2 = WC[:].rearrange("q (c f) -> q c f", c=2)
        wabd = sb("wabd", [128, 4, 128], BF16)
        wxbd = sb("wxbd", [128, 4, 128], BF16)
        sqb = sb("sqb", [128, KC, 256], BF16)
        lnv = sb("lnv", [128, 256])
        rstd = sb("rstd", [128, 256])
        nset_main = [(sqb[:], lnv[:], rstd[:], "m")]
        mixT = sb("mixT", [128, KC, TM], BF16)
        Sst = [sb(f"Sst{e}", [128, 2, 128]) for e in range(2)]
        hst = [sb(f"hst{e}", [128, 4]) for e in range(2)]
        xrh = [sb(f"xrh{e}", [128, 4, 3]) for e in range(2)]
        kTh = [sb(f"kTh{o}", [128, 4, 128], BF16) for o in range(2)]
        vbh = [sb(f"vbh{o}", [128, 256], BF16) for o in range(2)]

        AR_BYTES = 53504
        arena = sb("arena", [128, AR_BYTES // 4])
        ar_off = [0]

        def ar_reset():
            ar_off[0] = 0

        def ar(shape, dt=F32):
            esz = 4 if dt == F32 else 2
            n = 1
            for s_ in shape[1:]:
                n *= s_
            nb = (n * esz + 31) // 32 * 32
            o4 = ar_off[0] // 4
            ar_off[0] += nb
            assert ar_off[0] <= AR_BYTES, (ar_off[0], AR_BYTES)
            v = arena[0:shape[0], o4:o4 + nb // 4]
            if dt != F32:
                v = v.bitcast(dt)
            v = v[:, 0:n]
            if len(shape) == 3:
                v = v.rearrange("q (a b) -> q a b", a=shape[1])
            elif len(shape) == 4:
                v = v.rearrange("q (a b c) -> q a b c", a=shape[1], b=shape[2])
            return v

        ar_reset()
        xtok = [ar([128, D]) for _ in range(2)]
        ostg = ar([128, KC, 256])
        ar_reset()
        sgb = [ar([128, 2, 256]) for _ in range(2)]
        actb = [ar([128, 2, 256], BF16) for _ in range(2)]
        nset_ffn = nset_main + [(ar([128, KC, 256], BF16), ar([128, 256]), ar([128, 256]), "f%d" % i_) for i_ in range(2)]
        ar_reset()
        ptok = ar([128, T // 128, PLE], BF16)
        pT = ar([128, 2, T], BF16)
        gsb = [ar([128, 256]) for _ in range(4)]
        tmpA = [ar([128, 256]) for _ in range(4)]
        nset_ple = nset_main + [(ar([128, KC, 256], BF16), ar([128, 256]), ar([128, 256]), "p%d" % i_) for i_ in range(2)]
        ar_reset()
        S_all = ar([128, NCH, 2, 128], BF16)
        flT = ar([16, TM])
        wfup = ar([16, 256])
        sp_tok = ar([128, NTT, 256])
        edls = [ar([128, 256]) for _ in range(4)]
        k2 = ar([128, NTT, 256], BF16)
        v_tok = ar([128, NTT, 512], BF16)
        Eb = ar([128, 2, TM])
        Einv = ar([128, 2, TM])
        qdm = ar([128, 4, TM], BF16)
        kdT = ar([128, 2, TM], BF16)
        silr = ar([128, 4, TM], BF16)
        attm = ar([128, 4, 128], BF16)
        lsets = []
        for _ in range(4):
            lsets.append(dict(xrb=ar([128, TM + 3]), xc=ar([128, TM]), xcb=ar([128, TM], BF16), gel=ar([128, TM], BF16),
                              lt=[ar([128, TM]) for _ in range(4)]))
        ar_reset()
        qT = ar([128, 16, TM], BF16)
        kT = ar([128, 4, 128 + TM], BF16)
        vb = ar([128, NTT + 1, 256], BF16)
        ssets = []
        for _ in range(2):
            ssets.append(dict(s_sb=ar([128, 4, 256]), p_sb=ar([128, 4, 256], BF16), pTs=ar([128, 8, 128], BF16),
                              st4=ar([128, 8, 4])))
        o_sb = ar([128, 16, 64], BF16)
        nset_odd = nset_main + [(ar([128, KC, 256], BF16), ar([128, 256]), ar([128, 256]), "o%d" % i_) for i_ in range(2)]

        if SHOW_SBUF:
            print('sbuf bytes remaining', nc.sbuf_bytes_remaining)
        def mm(out, lhsT, rhs, start, stop, r, w):
            p.add("pe", lambda e: e.matmul(out, lhsT=lhsT, rhs=rhs, start=start, stop=stop), r=r, w=w)

        def tr(out, in_, idn, r, w):
            p.add("pe", lambda e: e.transpose(out, in_, idn), r=r, w=w)

        def act(out, in_, func, r, w, bias=None, scale=None, accum_out=None):
            kw = {}
            if bias is not None:
                kw["bias"] = bias
            if scale is not None:
                kw["scale"] = scale
            if accum_out is not None:
                kw["accum_out"] = accum_out
            p.add("act", lambda e: e.activation(out=out, in_=in_, func=func, **kw), r=r, w=w)

        def tt(out, in0, in1, op, r, w):
            p.add("dve", lambda e: e.tensor_tensor(out=out, in0=in0, in1=in1, op=op), r=r, w=w)

        def stt(out, in0, scalar, in1, op0, op1, r, w):
            p.add("dve", lambda e: e.scalar_tensor_tensor(out=out, in0=in0, scalar=scalar, in1=in1, op0=op0, op1=op1),
                  r=r, w=w)

        def ts(out, in0, s1, s2, op0, op1, r, w):
            p.add("dve", lambda e: e.tensor_scalar(out=out, in0=in0, scalar1=s1, scalar2=s2, op0=op0, op1=op1), r=r, w=w)

        def cp(out, in_, r, w):
            p.add("dve", lambda e: e.tensor_copy(out=out, in_=in_), r=r, w=w)

        def mset(ap, val, w):
            p.add("dve", lambda e: e.memset(ap, val), w=w)

        def dma(eng, out, in_, r, w, noextra=False):
            p.add(eng, lambda e: e.dma_start(out=out, in_=in_), r=r, w=w, dma=True, noextra=noextra)

        def barrier():
            p.add("dve", lambda e: e.memset(barc[:], 0.0), w=["AR"], noextra=True)
            p.extra = ["AR"]

        def interleave(gens):
            gens = list(gens)
            while gens:
                for g_ in list(gens):
                    try:
                        next(g_)
                    except StopIteration:
                        gens.remove(g_)

        def xk(c, t0, n):
            return [("x", c, t) for t in range(t0 // 256, (t0 + n + 255) // 256)]

        def hk(c, t0, n):
            return [("h", c, t) for t in range(t0 // 256, (t0 + n + 255) // 256)]

        def hk_all(t0, n):
            return [k for c in range(KC) for k in hk(c, t0, n)]

        def V(key, n=1, off=0):
            c = VM[key] + off
            return vecs[:, c:c + n]

        dma("sp", vecs[:], vecs_d, [], ["vecs"])
        dma("sp", rows[:], rows_d, [], ["rows"])
        dma("sp", cst[:], cst_d, [], ["cst"])
        dma("pool", bias_t[:], bias_d, [], ["bias_t"])
        dma("sp", bvb[:], bvb_d, [], ["bvb"])
        cp(cst_bf[:], cst[:], ["cst"], ["cst_bf"])
        for e in range(2):
            mset(Sst[e][:], 0.0, [("Sst", e, 0), ("Sst", e, 1)])
            mset(hst[e][:], 0.0, [("hst", e, c) for c in range(4)])
            mset(xrh[e][:], 0.0, [("xrh", e, c) for c in range(4)])
        for o in range(2):
            mset(kTh[o][:], 0.0, [("kTh", o)])
            mset(vbh[o][:], 0.0, [("vbh", o)])
        mset(wabd[:], 0.0, [("wabd", g) for g in range(8)])
        mset(wxbd[:], 0.0, [("wxbd", g) for g in range(8)])
        for e in range(2):
            lam = V(("lru_lambda", e), 4)
            act(dvec[:, 40:44], lam, AF.Exp, ["vecs"], ["dv_t0"], scale=-1.0)
            act(dvec[:, 44:48], dvec[:, 40:44], AF.Ln, ["dv_t0"], ["dv_t1"], bias=1.0)
            ts(dvec[:, e * 8:e * 8 + 4], dvec[:, 44:48], -8.0, None, ALU.mult, ALU.bypass, ["dv_t1"], ["dvec"])
            ts(dvec[:, e * 8 + 4:e * 8 + 8], dvec[:, 44:48], -16.0, None, ALU.mult, ALU.bypass, ["dv_t1"], ["dvec"])
        for o in range(2):
            ts(dvec[:, 16 + o * 8:24 + o * 8], V(("bq", o), 8), 0.125, None, ALU.mult, ALU.bypass, ["vecs"], ["dvec"])

        def rmsnorm(gkey, dst, dst_keys_fn, nsets=None):
            nsets = nsets or nset_main
            for ti, a in enumerate(range(0, T, 256)):
                sqb_, lnv_, rstd_, tag = nsets[ti % len(nsets)]
                bank = 7 - (ti % len(nsets))
                xr_keys = [k for c in range(KC) for k in xk(c, a, 256)]
                act(sqb_, xT[:, :, a:a + 256], AF.Square, xr_keys, [("sqb", tag)])
                for c in range(KC):
                    mm(ps[bank][:, 0:256], ones_bf, sqb_[:, c, :], c == 0, c == KC - 1, [("sqb", tag), "cst_bf"], [pk(bank)])
                act(lnv_, ps[bank][:, 0:256], AF.Ln, [pk(bank)], [("lnv", tag)], bias=EPS, scale=1.0 / D)
                act(rstd_, lnv_, AF.Exp, [("lnv", tag)], [("rstd", tag)], scale=-0.5)
                for c in range(KC):
                    stt(dst[:, c, a:a + 256], xT[:, c, a:a + 256], V(gkey, 1, c), rstd_, ALU.mult, ALU.mult,
                        xk(c, a, 256) + [("rstd", tag), "vecs"], dst_keys_fn(c, a, 256))

        ffn_seq = []
        for _ in range(NPASS):
            for l in range(L):
                for nm in ("ffn1", "ffn2"):
                    for g in range(NG):
                        ffn_seq.append((nm, l, g))
        ffn_issued = [0]

        def ffn_issue_loads(upto):
            while ffn_issued[0] < min(upto, len(ffn_seq)):
                i = ffn_issued[0]
                nm, l, g = ffn_seq[i]
                s = i % NSLOT
                src_g = W[nm + "_w_gate"][l].rearrange("(c q) f -> q c f", q=128)[:, :, g * 256:(g + 1) * 256]
                src_u = W[nm + "_w_up"][l].rearrange("(c q) f -> q c f", q=128)[:, :, g * 256:(g + 1) * 256]
                src_d = W[nm + "_w_down"][l][g * 256:(g + 1) * 256, :].rearrange("(j q) d -> q j d", q=128)
                dma("pool", wg[s][:], src_g, [], [("wg", s)], noextra=True)
                dma("pool", wu[s][:], src_u, [], [("wu", s)], noextra=True)
                dma("pool", wd[s][:], src_d, [], [("wd", s)], noextra=True)
                ffn_issued[0] += 1

        ffn_pos = [0]

        def ffn(nm, l):
            barrier()
            rmsnorm((nm + "_norm", l), hT, hk, nset_ffn)
            stages = [(g, t) for g in range(NG) for t in range(T // 256)]
            base = ffn_pos[0]
            ffn_issue_loads(base + NSLOT)

            def GU(si):
                g, t = stages[si]
                s = (base + g) % NSLOT
                for j in range(2):
                    bank = 2 * (si % 2) + j
                    for k in range(KC):
                        mm(ps[bank][:, 0:256], wg[s][:, k, j * 128:(j + 1) * 128], hT[:, k, t * 256:(t + 1) * 256],
                           k == 0, k == KC - 1, [("wg", s), ("h", k, t)], [pk(bank)])
                    for k in range(KC):
                        mm(ps[bank][:, 256:512], wu[s][:, k, j * 128:(j + 1) * 128], hT[:, k, t * 256:(t + 1) * 256],
                           k == 0, k == KC - 1, [("wu", s), ("h", k, t)], [pk(bank)])

            def ACTV(si):
                b = si % 2
                for j in range(2):
                    bank = 2 * b + j
                    act(sgb[b][:, j, :], ps[bank][:, 0:256], AF.Silu, [pk(bank)], [("sgb", b, j)])
                    tt(actb[b][:, j, :], sgb[b][:, j, :], ps[bank][:, 256:512], ALU.mult,
                       [pk(bank), ("sgb", b, j)], [("actb", b, j)])

            def DN(si):
                g, t = stages[si]
                s = (base + g) % NSLOT
                b = si % 2
                for d in range(KC):
                    bank = 4 + d // 2
                    cs = (d % 2) * 256
                    for j in range(2):
                        mm(ps[bank][:, cs:cs + 256], wd[s][:, j, d * 128:(d + 1) * 128], actb[b][:, j, :],
                           j == 0, j == 1, [("wd", s), ("actb", b, j)], [pk(bank)])
                for d in range(KC):
                    bank = 4 + d // 2
                    cs = (d % 2) * 256
                    stt(xT[:, d, t * 256:(t + 1) * 256], ps[bank][:, cs:cs + 256], 0.5, xT[:, d, t * 256:(t + 1) * 256],
                        ALU.mult, ALU.add, [pk(bank)] + xk(d, t * 256, 256), xk(d, t * 256, 256))
                if t == T // 256 - 1:
                    ffn_issue_loads(base + g + NSLOT + 1)

            GU(0)
            for si in range(len(stages)):
                if si + 1 < len(stages):
                    GU(si + 1)
                ACTV(si)
                DN(si)
            ffn_pos[0] += NG

        def ple_loads(l):
            dma("pool", WBv, W["ple_w_gate"][l].rearrange("(c q) f -> q c f", q=128), [], ["WB"], noextra=True)
            dma("pool", WCv2, W["ple_w_proj"][l].rearrange("(c q) f -> q c f", q=128), [], ["WC"], noextra=True)

        def ple(l, tok0):
            barrier()
            dma("pool", ptok, p_d[l, tok0:tok0 + T, :].rearrange("(i q) c -> q i c", q=128), [], ["ptok"])
            for i in range(T // 128):
                for c in range(2):
                    tr(psb(6)[:, c * 128:(c + 1) * 128], ptok[:, i, c * 128:(c + 1) * 128], ident_bf,
                       ["ptok", "cst_bf"], [pk(6)])
                cp(pT[:, :, i * 128:(i + 1) * 128], psb(6)[:, 0:256].rearrange("q (c t) -> q c t", c=2),
                   [pk(6)], [("pT", i)])
            rmsnorm(("ple_norm", l), hT, hk, nset_ple)
            for a in range(0, T, 256):
                t = a // 256
                for d in range(KC):
                    ba, bb = (d % 4) * 2, (d % 4) * 2 + 1
                    for k in range(KC):
                        mm(ps[ba][:, 0:256], WBv[:, k, d * 128:(d + 1) * 128], hT[:, k, a:a + 256], k == 0, k == KC - 1,
                           ["WB", ("h", k, t)], [pk(ba)])
                    for k in range(2):
                        mm(ps[bb][:, 0:256], WCv2[:, k, d * 128:(d + 1) * 128], pT[:, k, a:a + 256], k == 0, k == 1,
                           ["WC", ("pT", 2 * t), ("pT", 2 * t + 1)], [pk(bb)])
                    gs_, tm_ = gsb[d % 4], tmpA[d % 4]
                    act(gs_, ps[ba][:, 0:256], AF.Sigmoid, [pk(ba)], [("gsb", d % 4)])
                    tt(tm_, gs_, ps[bb][:, 0:256], ALU.mult, [("gsb", d % 4), pk(bb)], [("tmpA", d % 4)])
                    tt(xT[:, d, a:a + 256], xT[:, d, a:a + 256], tm_, ALU.add, [("tmpA", d % 4)] + xk(d, a, 256), xk(d, a, 256))

        wa1_sub = [("WA1", kvh, dup) for kvh in range(4) for dup in range(2)]
        wa1_all = ["WA1"] + wa1_sub

        def even_loads_gla(e):
            win = W["hyb_w_in"][e].rearrange("(c q) f -> q c f", q=128)
            dma("pool", WAv[:, :, 0:1024], win[:, :, 0:1024], [], ["WA0"], noextra=True)
            dma("pool", WAv[:, :, 1024:1552], win[:, :, 1024:1552], [], wa1_all, noextra=True)
            dma("pool", WBv, W["hyb_w_out"][e].rearrange("(c q) f -> q c f", q=128), [], ["WB"], noextra=True)
            for g in range(8):
                c, h2 = g // 2, g % 2
                dma("pool", wabd[h2 * 64:(h2 + 1) * 64, c, h2 * 64:(h2 + 1) * 64], W["lru_w_a"][e, g], [], [("wabd", g)],
                    noextra=True)
                dma("pool", wxbd[h2 * 64:(h2 + 1) * 64, c, h2 * 64:(h2 + 1) * 64], W["lru_w_x"][e, g], [], [("wxbd", g)],
                    noextra=True)

        def even_loads_lru(e):
            win = W["hyb_w_in"][e].rearrange("(c q) f -> q c f", q=128)
            dma("pool", WAv[:, :, 0:1024], win[:, :, 1552:2576], [], ["WA0"], noextra=True)

        def even_mixer(l, tok0):
            e = l // 2
            barrier()
            rmsnorm(("mix_norm", l), hT, hk)
            dma("sp", wfup, W["gla_w_fup"][e], [], ["wfup"])
            S = Sst[e]
            for m in range(NSUB if "gla" in PARTS else 0):
                a0 = m * TM
                hkeys = hk_all(a0, TM)

                def proj(bank, col0, ncol, wkeys):
                    for k in range(KC):
                        mm(ps[bank][0:ncol, 0:TM], WAv[:, k, col0:col0 + ncol], hT[:, k, a0:a0 + TM], k == 0, k == KC - 1,
                           wkeys + hkeys, [pk(bank)])

                for h in range(4):
                    orow = slice((1 - h % 2) * 64, (1 - h % 2) * 64 + 64)
                    mset(qdm[orow, h, :], 0.0, [("qdm", h)])
                proj(0, 1536, 16, wa1_all)
                cp(flT, ps[0][0:16, 0:TM], [pk(0)], ["flT"])
                if GLA_STOP == 1:
                    continue
                def tile_gen(i):
                    tk = a0 + i * 128
                    pb = 1 if i % 2 == 0 else 0
                    db = 4 if i % 2 == 0 else 7
                    e1, e2 = edls[(i % 2) * 2], edls[(i % 2) * 2 + 1]
                    k1, k2_ = ("edl", (i % 2) * 2), ("edl", (i % 2) * 2 + 1)
                    mm(ps[pb][:, 0:256], flT[0:16, i * 128:(i + 1) * 128], wfup[0:16, :], True, False, ["flT", "wfup"], [pk(pb)])
                    mm(ps[pb][:, 0:256], ones[0:1, :], rows[0:1, e * 256:(e + 1) * 256], False, True, ["cst", "rows"], [pk(pb)])
                    yield
                    act(e1, ps[pb][:, 0:256], AF.Exp, [pk(pb)], [k1], scale=-1.0)
                    yield
                    act(sp_tok[:, i, :], e1, AF.Ln, [k1], [("sp_tok", i)], bias=1.0)
                    for k in range(KC):
                        mm(ps[pb][:, 0:256], hT[:, k, tk:tk + 128], WAv[:, k, 256:512], k == 0, k == KC - 1,
                           ["WA0"] + hkeys, [pk(pb)])
                    yield
                    for hp in range(2):
                        mm(ps[2 + hp][:, i * 128:(i + 1) * 128], sp_tok[:, i, hp * 128:(hp + 1) * 128], TriC, True, True,
                           [("sp_tok", i), "cst"], [pk(2 + hp)])
                    mm(ps[db][:, 0:256], UpC, sp_tok[:, i, :], True, True, [("sp_tok", i), "cst"], [pk(db)])
                    yield
                    act(e2, ps[db][:, 0:256], AF.Exp, [pk(db)], [k2_])
                    yield
                    tt(k2[:, i, :], ps[pb][:, 0:256], e2, ALU.mult, [pk(pb), k2_], [("k2", i)])
                    for k in range(KC):
                        mm(ps[db][:, 0:512], hT[:, k, tk:tk + 128], WAv[:, k, 512:1024], k == 0, k == KC - 1,
                           ["WA0"] + hkeys, [pk(db)])
                    yield
                    act(v_tok[:, i, :], ps[db][:, 0:512], AF.Copy, [pk(db)], [("v_tok", i)])
                    yield

                for i2 in range(0, NTT, 2):
                    interleave([tile_gen(i) for i in range(i2, min(i2 + 2, NTT))])
                if GLA_STOP == 2:
                    continue
                for hp in range(2):
                    act(Eb[:, hp, :], ps[2 + hp][:, 0:TM], AF.Exp, [pk(2 + hp)], [("Eb", hp)])
                    act(Einv[:, hp, :], ps[2 + hp][:, 0:TM], AF.Exp, [pk(2 + hp)], [("Einv", hp)], scale=-1.0)
                def gen_proj():
                    for hp in range(2):
                        proj(0, hp * 128, 128, ["WA0"])
                        for h2 in range(2):
                            pr = slice(h2 * 64, (h2 + 1) * 64)
                            stt(qdm[pr, 2 * hp + h2, :], ps[0][pr, 0:TM], 0.125, Eb[pr, hp, :], ALU.mult, ALU.mult,
                                [pk(0), ("Eb", hp)], [("qdm", 2 * hp + h2)])
                        yield
                        proj(1, 256 + hp * 128, 128, ["WA0"])
                        tt(kdT[:, hp, :], ps[1][:, 0:TM], Einv[:, hp, :], ALU.mult, [pk(1), ("Einv", hp)], [("kdT", hp)])
                        yield
                    for c in range(4):
                        proj(c % 2, 1024 + c * 128, 128, wa1_all)
                        act(silr[:, c, :], ps[c % 2][:, 0:TM], AF.Silu, [pk(c % 2)], [("silr", c)])
                        yield

                def gen_rec():
                    for n in range(NCH):
                        i, c2 = n // 2, n % 2
                        rsl = slice(c2 * 64, (c2 + 1) * 64)
                        for hp in range(2):
                            cp(S_all[:, n, hp, :], S[:, hp, :], [("Sst", e, hp)], [("S_all", n, hp)])
                            mm(ps[4 + hp][:, 0:256], k2[rsl, i, hp * 128:(hp + 1) * 128], v_tok[rsl, i, hp * 256:(hp + 1) * 256],
                               True, True, [("k2", i), ("v_tok", i)], [pk(4 + hp)])
                            yield
                            for h2 in range(2):
                                pr = slice(h2 * 64, (h2 + 1) * 64)
                                stt(S[pr, hp, :], S[pr, hp, :], Eb[pr, hp, n * 64 + 63:n * 64 + 64],
                                    ps[4 + hp][pr, h2 * 128:(h2 + 1) * 128], ALU.mult, ALU.add,
                                    [("Sst", e, hp), ("Eb", hp), pk(4 + hp)], [("Sst", e, hp)])
                            yield

                interleave([gen_proj(), gen_rec()])
                for i in range(NTT):
                    tsl = slice(i * 128, (i + 1) * 128)
                    for h in range(4):
                        hp, pr = h // 2, slice((h % 2) * 64, (h % 2) * 64 + 64)
                        mm(ps[6][:, h * 128:(h + 1) * 128], kdT[:, hp, tsl], qdm[:, h, tsl], True, True,
                           [("kdT", hp), ("qdm", h)], [pk(6)])
                    for h in range(4):
                        tt(attm[:, h, :], ps[6][:, h * 128:(h + 1) * 128], maskA_bf, ALU.mult, [pk(6), "cst_bf"], [("attm", h)])
                    for h in range(4):
                        hp, pr = h // 2, slice((h % 2) * 64, (h % 2) * 64 + 64)
                        mm(ps[h][:, tsl], v_tok[:, i, h * 128:(h + 1) * 128], attm[:, h, :], True, False,
                           [("v_tok", i), ("attm", h)], [pk(h)])
                        for c2 in range(2):
                            n = 2 * i + c2
                            csl = slice(i * 128 + c2 * 64, i * 128 + c2 * 64 + 64)
                            mm(ps[h][:, csl], S_all[:, n, hp, :], qdm[:, h, csl], False, c2 == 1,
                               [("S_all", n, hp), ("qdm", h)], [pk(h)])
                if GLA_STOP == 5:
                    continue
                def head_norm(h):
                    sqh, lnh, rsh, t1 = lsets[h]["lt"]
                    K_ = lambda nm: (nm, h)
                    nb = 4 + h
                    act(sqh, ps[h][:, 0:TM], AF.Square, [pk(h)], [K_("lt0")])
                    yield
                    mm(ps[nb][:, 0:TM], ones, sqh, True, True, [K_("lt0"), "cst"], [pk(nb)])
                    yield
                    act(lnh, ps[nb][:, 0:TM], AF.Ln, [pk(nb)], [K_("lt1")], bias=EPS, scale=1.0 / 128)
                    yield
                    act(rsh, lnh, AF.Exp, [K_("lt1")], [K_("lt2")], scale=-0.5)
                    yield
                    tt(t1, ps[h][:, 0:TM], rsh, ALU.mult, [pk(h), K_("lt2")], [K_("lt3")])
                    yield
                    stt(mixT[:, h, :], t1, V(("gla_norm", e), 1, h), silr[:, h, :], ALU.mult, ALU.mult,
                        [K_("lt3"), "vecs", ("silr", h)], [("mixT", h)])
                    yield

                interleave([head_norm(h) for h in range(4)])
                for d in range(KC):
                    bank = 4 + d % 2
                    for k in range(4):
                        mm(ps[bank][:, 0:TM], WBv[:, k, d * 128:(d + 1) * 128], mixT[:, k, :], k == 0, k == 3,
                           ["WB", ("mixT", k)], [pk(bank)])
                    tt(xT[:, d, a0:a0 + TM], xT[:, d, a0:a0 + TM], ps[bank][:, 0:TM], ALU.add,
                       [pk(bank)] + xk(d, a0, TM), xk(d, a0, TM))
            even_loads_lru(e)
            cneg = dvec[:, e * 8:e * 8 + 4]
            cneg2 = dvec[:, e * 8 + 4:e * 8 + 8]
            for m in range(NSUB if "lru" in PARTS else 0):
                a0 = m * TM
                hkeys = hk_all(a0, TM)
                def lru_block(c, si):
                    L_ = lsets[si]
                    xrb, xc, xcb, gel, lt = L_["xrb"], L_["xc"], L_["xcb"], L_["gel"], L_["lt"]
                    K_ = lambda nm: (nm, si)
                    b0, b1 = 2 * si, 2 * si + 1
                    for k in range(KC):
                        mm(ps[b0][:, 0:TM], WAv[:, k, c * 128:(c + 1) * 128], hT[:, k, a0:a0 + TM], k == 0, k == KC - 1,
                           ["WA0"] + hkeys, [pk(b0)])
                    cp(xrb[:, 0:3], xrh[e][:, c, :], [("xrh", e, c)], [K_("xrb")])
                    act(xrb[:, 3:3 + TM], ps[b0][:, 0:TM], AF.Copy, [pk(b0)], [K_("xrb")])
                    yield
                    for k in range(KC):
                        mm(ps[b1][:, 0:TM], WAv[:, k, 512 + c * 128:512 + (c + 1) * 128], hT[:, k, a0:a0 + TM],
                           k == 0, k == KC - 1, ["WA0"] + hkeys, [pk(b1)])
                    act(lt[0], ps[b1][:, 0:TM], AF.Square, [pk(b1)], [K_("lt0")])
                    yield
                    ts(lt[0], lt[0], 0.044715, 1.0, ALU.mult, ALU.add, [K_("lt0")], [K_("lt0")])
                    tt(lt[0], lt[0], ps[b1][:, 0:TM], ALU.mult, [K_("lt0"), pk(b1)], [K_("lt0")])
                    yield
                    act(lt[1], lt[0], AF.Sigmoid, [K_("lt0")], [K_("lt1")], scale=1.5957691216057308)
                    yield
                    tt(gel, lt[1], ps[b1][:, 0:TM], ALU.mult, [K_("lt1"), pk(b1)], [K_("gel")])
                    cw = VM[("conv_w", e)] + c * 4
                    ts(xc, xrb[:, 0:TM], vecs[:, cw:cw + 1], V(("conv_b", e), 1, c), ALU.mult, ALU.add,
                       [K_("xrb"), "vecs"], [K_("xc")])
                    yield
                    for tap in range(1, 4):
                        stt(xc, xrb[:, tap:tap + TM], vecs[:, cw + tap:cw + tap + 1], xc, ALU.mult, ALU.add,
                            [K_("xrb"), K_("xc"), "vecs"], [K_("xc")])
                        yield
                    cp(xrh[e][:, c, :], xrb[:, TM:TM + 3], [K_("xrb")], [("xrh", e, c)])
                    act(xcb, xc, AF.Copy, [K_("xc")], [K_("xcb")])
                    yield
                    mm(ps[b0][:, 0:TM], wabd[:, c, :], xcb, True, True, [("wabd", 2 * c), ("wabd", 2 * c + 1), K_("xcb")], [pk(b0)])
                    mm(ps[b1][:, 0:TM], wxbd[:, c, :], xcb, True, True, [("wxbd", 2 * c), ("wxbd", 2 * c + 1), K_("xcb")], [pk(b1)])
                    act(lt[2], ps[b0][:, 0:TM], AF.Sigmoid, [pk(b0), "vecs"], [K_("lt2")], bias=V(("lru_b_a", e), 1, c))
                    act(lt[3], ps[b1][:, 0:TM], AF.Sigmoid, [pk(b1), "vecs"], [K_("lt3")], bias=V(("lru_b_x", e), 1, c))
                    yield
                    act(lt[0], lt[2], AF.Exp, [K_("lt2"), "dvec"], [K_("lt0")], scale=cneg[:, c:c + 1])
                    act(lt[1], lt[2], AF.Exp, [K_("lt2"), "dvec"], [K_("lt1")], scale=cneg2[:, c:c + 1])
                    tt(lt[3], lt[3], xc, ALU.mult, [K_("lt3"), K_("xc")], [K_("lt3")])
                    yield
                    ts(lt[1], lt[1], -1.0, 1.0, ALU.mult, ALU.add, [K_("lt1")], [K_("lt1")])
                    yield
                    act(lt[1], lt[1], AF.Sqrt, [K_("lt1")], [K_("lt1")])
                    yield
                    tt(lt[3], lt[3], lt[1], ALU.mult, [K_("lt3"), K_("lt1")], [K_("lt3")])
                    yield
                    p.add("dve", lambda eng: eng.tensor_tensor_scan(out=lt[2], data0=lt[0], data1=lt[3],
                                                                    initial=hst[e][:, c:c + 1], op0=ALU.mult, op1=ALU.add),
                          r=[K_("lt0"), K_("lt3"), ("hst", e, c)], w=[K_("lt2")])
                    yield
                    cp(hst[e][:, c:c + 1], lt[2][:, TM - 1:TM], [K_("lt2")], [("hst", e, c)])
                    tt(mixT[:, 4 + c, :], lt[2], gel, ALU.mult, [K_("lt2"), K_("gel")], [("mixT", 4 + c)])
                    yield

                interleave([lru_block(c, c) for c in range(4)])
                for d in range(KC):
                    bank = 4 + d % 2
                    for k in range(4):
                        mm(ps[bank][:, 0:TM], WBv[:, 4 + k, d * 128:(d + 1) * 128], mixT[:, 4 + k, :], k == 0, k == 3,
                           ["WB", ("mixT", 4 + k)], [pk(bank)])
                    tt(xT[:, d, a0:a0 + TM], xT[:, d, a0:a0 + TM], ps[bank][:, 0:TM], ALU.add,
                       [pk(bank)] + xk(d, a0, TM), xk(d, a0, TM))

        def odd_loads(o):
            wqkv = W["swa_w_qkv"][o].rearrange("(c q) f -> q c f", q=128)
            dma("pool", WAv[:, :, 0:1024], wqkv[:, :, 0:1024], [], ["WA0"], noextra=True)
            for kvh in range(4):
                for dup in range(2):
                    c0 = 1024 + kvh * 128 + dup * 64
                    dma("pool", WAv[:, :, c0:c0 + 64], wqkv[:, :, 1024 + kvh * 64:1024 + kvh * 64 + 64], [],
                        [("WA1", kvh, dup)], noextra=True)
            dma("pool", WBv, W["swa_w_o"][o].rearrange("(c q) f -> q c f", q=128), [], ["WB"], noextra=True)
            dma("pool", WCv8, wqkv[:, :, 1280:1536], [], ["WC"], noextra=True)

        def odd_mixer(l, tok0):
            o = l // 2
            barrier()
            rmsnorm(("mix_norm", l), hT, hk, nset_odd)
            for m in range(NSUB):
                a0 = m * TM
                hkeys = hk_all(a0, TM)
                for blk in range(8):
                    bank = 6 + blk % 2
                    for k in range(KC):
                        mm(ps[bank][:, 0:TM], WAv[:, k, blk * 128:(blk + 1) * 128], hT[:, k, a0:a0 + TM], k == 0, k == KC - 1,
                           ["WA0"] + hkeys, [pk(bank)])
                    for h2 in range(2):
                        pr = slice(h2 * 64, (h2 + 1) * 64)
                        orow = slice((1 - h2) * 64, (1 - h2) * 64 + 64)
                        mset(qT[orow, 2 * blk + h2, :], 0.0, [("qT", 2 * blk + h2)])
                        act(qT[pr, 2 * blk + h2, :], ps[bank][pr, 0:TM], AF.Identity, [pk(bank), "dvec"],
                            [("qT", 2 * blk + h2)], bias=dvec[pr, 16 + o * 8 + blk:16 + o * 8 + blk + 1], scale=0.125)
                if ODD_STOP == 1:
                    continue
                cp(kT[:, :, 0:128], kTh[o][:], [("kTh", o)], ["kT"])
                cp(vb[:, 0, :], vbh[o][:], [("vbh", o)], [("vb", 0)])
                for kvh in range(4):
                    bank = 6 + kvh % 2
                    for k in range(KC):
                        mm(ps[bank][:, 0:TM], WAv[:, k, 1024 + kvh * 128:1024 + (kvh + 1) * 128], hT[:, k, a0:a0 + TM],
                           k == 0, k == KC - 1, [("WA1", kvh, 0), ("WA1", kvh, 1)] + hkeys, [pk(bank)])
                    act(kT[:, kvh, 128:128 + TM], ps[bank][:, 0:TM], AF.Identity, [pk(bank), "vecs"], ["kT"],
                        bias=V(("bk", o), 1, kvh))
                for i in range(NTT):
                    tk = a0 + i * 128
                    bank = 6 + i % 2
                    for k in range(KC):
                        mm(ps[bank][:, 0:256], hT[:, k, tk:tk + 128], WCv8[:, k, :], k == 0, k == KC - 1, ["WC"] + hkeys,
                           [pk(bank)])
                    tt(vb[:, 1 + i, :], ps[bank][:, 0:256], bvb[:, o, :], ALU.add, [pk(bank), "bvb"], [("vb", 1 + i)])
                cp(kTh[o][:], kT[:, :, TM:TM + 128], ["kT"], [("kTh", o)])
                cp(vbh[o][:], vb[:, NTT, :], [("vb", NTT)], [("vbh", o)])
                if ODD_STOP == 2:
                    continue
                for i in range(NTT):
                    gb = (tok0 + a0) // 128 + i
                    qsl = slice(i * 128, (i + 1) * 128)
                    def swa_kvh(kvh, si):
                        S_ = ssets[si]
                        s_sb, p_sb, pTs, st4 = S_["s_sb"], S_["p_sb"], S_["pTs"], S_["st4"]
                        negmx, rsum, t4, es, den, rden = (st4[:, ii, :] for ii in range(6))
                        K_ = lambda nm: (nm, si)
                        sb0 = (0, 1) if si == 0 else (6, 7)
                        tb = 2 if si == 0 else 5
                        for g in range(4):
                            h = 4 * kvh + g
                            bk_ = sb0[g // 2]
                            mm(ps[bk_][:, (g % 2) * 256:(g % 2) * 256 + 256], qT[:, h, qsl],
                               kT[:, kvh, i * 128:i * 128 + 256], True, True, [("qT", h), "kT"], [pk(bk_)])
                        yield
                        for hb in range(2):
                            tt(s_sb[:, 2 * hb:2 * hb + 2, :], ps[sb0[hb]][:, 0:512].rearrange("q (g t) -> q g t", g=2),
                               bias_t[:, 4 * kvh + 2 * hb:4 * kvh + 2 * hb + 2, :], ALU.add, [pk(sb0[hb]), "bias_t"], [K_("s_sb")])
                        if gb == 0:
                            mset(s_sb[:, :, 0:128], NEG, [K_("s_sb")])
                        yield
                        p.add("dve", lambda eng: eng.tensor_reduce(out=negmx, in_=s_sb, axis=AX.X, op=ALU.max, negate=True),
                              r=[K_("s_sb")], w=[K_("negmx")])
                        yield
                        for g in range(4):
                            act(p_sb[:, g, :], s_sb[:, g, :], AF.Exp, [K_("s_sb"), K_("negmx")], [("p_sb", si, g), ("rsum", si, g)],
                                bias=negmx[:, g:g + 1], accum_out=rsum[:, g:g + 1])
                        tt(t4, negmx, V(("sinks", o), 4, 4 * kvh), ALU.add, [K_("negmx"), "vecs"], [K_("t4")])
                        yield
                        act(es, t4, AF.Exp, [K_("t4")], [K_("es")])
                        for g in range(4):
                            for half in range(2):
                                tr(psb(tb)[:, (g * 2 + half) * 128:(g * 2 + half + 1) * 128],
                                   p_sb[:, g, half * 128:(half + 1) * 128], ident_bf, [("p_sb", si, g), "cst_bf"], [pk(tb)])
                        yield
                        tt(den, rsum, es, ALU.add, [("rsum", si, g) for g in range(4)] + [K_("es")], [K_("den")])
                        cp(pTs, psb(tb)[:, 0:1024].rearrange("q (a t) -> q a t", a=8), [pk(tb)], [K_("pTs")])
                        yield
                        p.add("dve", lambda eng: eng.reciprocal(out=rden, in_=den), r=[K_("den")], w=[K_("rden")])
                        for g in range(4):
                            h = 4 * kvh + g
                            ob = 3 + h // 8
                            for half in range(2):
                                mm(ps[ob][:, (h % 8) * 64:(h % 8) * 64 + 64], pTs[:, g * 2 + half, :],
                                   vb[:, i + half, kvh * 64:(kvh + 1) * 64], half == 0, half == 1,
                                   [K_("pTs"), ("vb", i + half)], [pk(ob)])
                        yield
                        ob = 3 + kvh // 2
                        tt(o_sb[:, 4 * kvh:4 * kvh + 4, :],
                           ps[ob][:, (kvh % 2) * 256:(kvh % 2) * 256 + 256].rearrange("q (g d) -> q g d", g=4),
                           rden.unsqueeze(2).to_broadcast([128, 4, 64]), ALU.mult, [pk(ob), K_("rden")],
                           [("o_sb", 2 * kvh), ("o_sb", 2 * kvh + 1)])
                        yield

                    for kpair in ((0, 2), (1, 3)):
                        interleave([swa_kvh(kpair[0], 0), swa_kvh(kpair[1], 1)])
                    if ODD_STOP == 7:
                        continue
                    for b2 in range(8):
                        tr(psb(2)[:, b2 * 128:(b2 + 1) * 128], o_sb[:, 2 * b2:2 * b2 + 2, :].rearrange("q a d -> q (a d)"),
                           ident_bf, [("o_sb", b2), "cst_bf"], [pk(2)])
                    cp(mixT[:, :, qsl], psb(2)[:, 0:1024].rearrange("q (a t) -> q a t", a=8), [pk(2)],
                       [("mixT", k) for k in range(8)])
                for d in range(KC):
                    bank = 6 + d % 2
                    for k in range(KC):
                        mm(ps[bank][:, 0:TM], WBv[:, k, d * 128:(d + 1) * 128], mixT[:, k, :], k == 0, k == KC - 1,
                           ["WB", ("mixT", k)], [pk(bank)])
                    stt(xT[:, d, a0:a0 + TM], ps[bank][:, 0:TM], V(("bo", o), 1, d), xT[:, d, a0:a0 + TM], ALU.add, ALU.add,
                        [pk(bank), "vecs"] + xk(d, a0, TM), xk(d, a0, TM))

        for ps_i in range(NPASS):
            tok0 = ps_i * T
            barrier()
            for i in range(T // 128):
                xb_ = xtok[i % 2]
                dma("sp", xb_, x_d[tok0 + i * 128:tok0 + (i + 1) * 128, :], [], [("xtok", i % 2)])
                for half in range(2):
                    for c in range(4):
                        tr(ps[half][:, c * 128:(c + 1) * 128], xb_[:, (half * 4 + c) * 128:(half * 4 + c + 1) * 128], ident,
                           [("xtok", i % 2), "cst"], [pk(half)])
                    cp(xT[:, half * 4:half * 4 + 4, i * 128:(i + 1) * 128],
                       ps[half][:, 0:512].rearrange("q (c t) -> q c t", c=4), [pk(half)],
                       [k for c in range(half * 4, half * 4 + 4) for k in xk(c, i * 128, 128)])
            for l in range(L):
                if "mix" in PARTS:
                    if l % 2 == 0:
                        even_loads_gla(l // 2)
                    else:
                        odd_loads(l // 2)
                if "ffn1" in PARTS:
                    ffn("ffn1", l)
                else:
                    ffn_pos[0] += NG
                    ffn_issued[0] = max(ffn_issued[0], ffn_pos[0])
                if "mix" in PARTS:
                    if l % 2 == 0:
                        even_mixer(l, tok0)
                    else:
                        odd_mixer(l, tok0)
                if "ple" in PARTS:
                    ple_loads(l)
                if "ffn2" in PARTS:
                    ffn("ffn2", l)
                else:
                    ffn_pos[0] += NG
                    ffn_issued[0] = max(ffn_issued[0], ffn_pos[0])
                if "ple" in PARTS:
                    ple(l, tok0)
            barrier()
            for a in range(0, T, 256):
                xr_keys = [k for c in range(KC) for k in xk(c, a, 256)]
                act(ostg, xT[:, :, a:a + 256], AF.Square, xr_keys, ["sq"])
                for c in range(KC):
                    mm(ps[7][:, 0:256], ones, ostg[:, c, :], c == 0, c == KC - 1, ["sq", "cst"], [pk(7)])
                act(lnv[:], ps[7][:, 0:256], AF.Ln, [pk(7)], ["lnv"], bias=EPS, scale=1.0 / D)
                act(rstd[:], lnv[:], AF.Exp, ["lnv"], ["rstd"], scale=-0.5)
                for c in range(KC):
                    stt(ostg[:, c, :], xT[:, c, a:a + 256], V("final_norm", 1, c), rstd[:], ALU.mult, ALU.mult,
                        xk(c, a, 256) + ["rstd", "vecs", "sq"], ["sq"])
                for i in range(2):
                    ob_ = xtok[i % 2]
                    for half in range(2):
                        for c in range(4):
                            tr(ps[half][:, c * 128:(c + 1) * 128], ostg[:, half * 4 + c, i * 128:(i + 1) * 128], ident,
                               ["sq", "cst"], [pk(half)])
                        cp(ob_[:, half * 512:(half + 1) * 512], ps[half][:, 0:512], [pk(half)], [("xtok", i % 2)])
                    r0 = tok0 + a + i * 128
                    dma("sp", out_d[r0:r0 + 128, :], ob_, [("xtok", i % 2)], [("out", r0)])
        p.add("sp", None, r=[("out", r0) for r0 in range(0, S, 128)])
        p.emit(nc)
    return nc


SHOW_SBUF = False
GLA_STOP = 0
ODD_STOP = 0
PARTS = {"ffn1", "mix", "ffn2", "ple", "gla", "lru"}
S_FULL = 4096
T_PASS = 1024
T_MIX = 256
N_CORES = 4

_W_NAMES = ["ffn1_w_gate", "ffn1_w_up", "ffn1_w_down", "ffn2_w_gate", "ffn2_w_up", "ffn2_w_down",
            "ple_w_proj", "ple_w_gate", "hyb_w_in", "hyb_w_out", "gla_w_fup", "lru_w_a", "lru_w_x",
            "swa_w_qkv", "swa_w_o"]


def make_in_maps(inp, n_seq, S):
    vecs, rows, bias_t, cst = host_layout(inp)
    bvb = np.ascontiguousarray(np.broadcast_to(rows[0, 512:1024].reshape(1, 2, 256), (128, 2, 256)))
    shared = {"vecs": vecs, "rows": rows, "bias_t": bias_t, "cst": cst, "bvb": bvb}
    for nm in _W_NAMES:
        shared[nm] = np.ascontiguousarray(np.asarray(inp[nm], np.float32))
    maps = []
    for b in range(n_seq):
        m = dict(shared)
        m["x"] = np.ascontiguousarray(np.asarray(inp["x"][b, :S], np.float32))
        m["p"] = np.ascontiguousarray(np.asarray(inp["p"][:, b, :S], np.float32))
        maps.append(m)
    return maps


def kernel(**inputs):
    nc = build(S_FULL, T_PASS, T_MIX)
    maps = make_in_maps(inputs, N_CORES, S_FULL)
    res = run_bass_kernel_spmd(nc, maps, core_ids=list(range(N_CORES)))
    return np.stack([np.asarray(r["out"], np.float32) for r in res.results], axis=0)
```

```python
import contextlib
import numpy as np
import concourse.bass as bass
import concourse.mybir as mybir
from concourse.bass_utils import run_bass_kernel_spmd

F32 = mybir.dt.float32
BF16 = mybir.dt.bfloat16
AF = mybir.ActivationFunctionType
ALU = mybir.AluOpType
AX = mybir.AxisListType

D = 1024
KC = 8
DFF = 2816
NG = 11
DEPTH = 4
PLE = 256
HYB_IN = 2576
EPS = 1e-6
NEG = -30000.0

SEG = 30000
NDSEM = 24
NDSEM_Q = {"pool": 8}


class Op:
    __slots__ = ("eng", "fn", "deps", "sig", "idx", "dma", "dsem", "dval")

    def __init__(self, eng, fn, dma):
        self.eng = eng
        self.fn = fn
        self.dma = dma
        self.deps = ()
        self.sig = False
        self.idx = -1
        self.dsem = -1
        self.dval = 0


class Prog:
    ENGS = ("pe", "act", "dve", "pool", "sp")

    def __init__(self, same_eng_sync=True):
        self.q = {e: [] for e in self.ENGS}
        self.lastw = {}
        self.readers = {}
        self.same = same_eng_sync
        self.ndma = {e: 0 for e in self.ENGS}
        self.extra = []

    def add(self, eng, fn, r=(), w=(), dma=False, noextra=False):
        if self.extra and not noextra:
            r = list(r) + self.extra
        op = Op(eng, fn, dma)
        deps = {}
        for k in r:
            p = self.lastw.get(k)
            if p is not None:
                deps[id(p)] = p
        for k in w:
            p = self.lastw.get(k)
            if p is not None:
                deps[id(p)] = p
            rd = self.readers.get(k)
            if rd:
                for p in rd.values():
                    deps[id(p)] = p
        out = []
        for p in deps.values():
            if p is op:
                continue
            if not p.dma and not dma and p.eng == eng:
                if eng == "pe" or not self.same:
                    continue
            if not p.dma:
                p.sig = True
            out.append(p)
        op.deps = out
        for k in w:
            self.lastw[k] = op
            self.readers[k] = {}
        for k in r:
            d = self.readers.get(k)
            if d is None:
                d = self.readers[k] = {}
            if dma:
                d[("dma", id(op))] = op
            else:
                d[eng] = op
        if dma:
            n = self.ndma[eng]
            self.ndma[eng] = n + 1
            nd = NDSEM_Q.get(eng, NDSEM)
            op.dsem = n % nd
            op.dval = 16 * (n // nd + 1)
        self.q[eng].append(op)
        return op

    def emit(self, nc):
        nsig = {}
        for e in self.ENGS:
            c = 0
            for op in self.q[e]:
                if op.sig and not op.dma:
                    op.idx = c
                    c += 1
            nsig[e] = c
        with contextlib.ExitStack() as st:
            esem = {}
            for e in self.ENGS:
                esem[e] = [st.enter_context(nc.semaphore(f"s_{e}_{i}"))
                           for i in range(nsig[e] // SEG + 1)]
            dsem = {}
            for e in self.ENGS:
                if self.ndma[e]:
                    dsem[e] = [st.enter_context(nc.semaphore(f"d_{e}_{i}"))
                               for i in range(min(NDSEM_Q.get(e, NDSEM), self.ndma[e]))]
            block = st.enter_context(nc.Block())

            def run(ename, eng):
                seen_e = {e: -1 for e in self.ENGS}
                seen_d = {}
                for op in self.q[ename]:
                    for p in op.deps:
                        if p.dma:
                            key = (p.eng, p.dsem)
                            if seen_d.get(key, 0) >= p.dval:
                                continue
                            eng.wait_ge(dsem[p.eng][p.dsem], p.dval)
                            seen_d[key] = p.dval
                        else:
                            if seen_e[p.eng] >= p.idx:
                                continue
                            eng.wait_ge(esem[p.eng][p.idx // SEG], p.idx % SEG + 1)
                            seen_e[p.eng] = p.idx
                    if op.dma:
                        key = (ename, op.dsem)
                        if op.dval > 16 and seen_d.get(key, 0) < op.dval - 16:
                            eng.wait_ge(dsem[ename][op.dsem], op.dval - 16)
                            seen_d[key] = op.dval - 16
                        ins = op.fn(eng)
                        ins.then_inc(dsem[ename][op.dsem], 16)
                    elif op.fn is not None:
                        ins = op.fn(eng)
                        if op.sig:
                            ins.then_inc(esem[ename][op.idx // SEG], 1)
                n = self.ndma[ename]
                nd = NDSEM_Q.get(ename, NDSEM)
                for i in range(min(nd, n)):
                    final = 16 * ((n - 1 - i) // nd + 1)
                    if seen_d.get((ename, i), 0) < final:
                        eng.wait_ge(dsem[ename][i], final)

            @block.tensor
            def _(eng):
                run("pe", eng)

            @block.scalar
            def _(eng):
                run("act", eng)

            @block.vector
            def _(eng):
                run("dve", eng)

            @block.gpsimd
            def _(eng):
                run("pool", eng)

            @block.sync
            def _(eng):
                run("sp", eng)


def _vec_map():
    m = {}
    c = 0

    def put(name, n):
        nonlocal c
        m[name] = c
        c += n

    for l in range(DEPTH):
        for nm in ("ffn1_norm", "mix_norm", "ffn2_norm", "ple_norm"):
            put((nm, l), 8)
    put("final_norm", 8)
    for e in range(2):
        put(("gla_norm", e), 4)
        put(("conv_w", e), 16)
        put(("conv_b", e), 4)
        put(("lru_b_a", e), 4)
        put(("lru_b_x", e), 4)
        put(("lru_lambda", e), 4)
    for o in range(2):
        put(("bq", o), 8)
        put(("bk", o), 4)
        put(("bo", o), 8)
        put(("sinks", o), 16)
    m["_n"] = c
    return m


VM = _vec_map()


def _t5_bucket_np(dist):
    max_exact = 16
    d = np.maximum(dist, 1).astype(np.float32)
    large = max_exact + (np.log(d / max_exact) / np.float32(np.log(128 / max_exact)) * (32 - max_exact)).astype(np.int32)
    large = np.minimum(large, 31)
    return np.where(dist < max_exact, dist, large)


def _colvec(v):
    return np.ascontiguousarray(np.asarray(v, np.float32).reshape(-1, 128).T)


def host_layout(inp):
    vecs = np.zeros((128, VM["_n"]), np.float32)

    def put(key, arr):
        c = VM[key]
        vecs[:, c:c + arr.shape[1]] = arr

    for l in range(DEPTH):
        for nm in ("ffn1_norm", "mix_norm", "ffn2_norm", "ple_norm"):
            put((nm, l), _colvec(inp[nm][l]))
    put("final_norm", _colvec(inp["final_norm"]))
    for e in range(2):
        put(("gla_norm", e), _colvec(inp["gla_norm"][e]))
        cw = np.asarray(inp["lru_conv_w"][e], np.float32)
        cwt = cw.reshape(4, 4, 128).transpose(2, 1, 0).reshape(128, 16)
        put(("conv_w", e), cwt)
        put(("conv_b", e), _colvec(inp["lru_conv_b"][e]))
        put(("lru_b_a", e), _colvec(inp["lru_b_a"][e]))
        put(("lru_b_x", e), _colvec(inp["lru_b_x"][e]))
        put(("lru_lambda", e), _colvec(inp["lru_lambda"][e]))
    for o in range(2):
        b = np.asarray(inp["swa_b_qkv"][o], np.float32)
        put(("bq", o), _colvec(b[:1024]))
        bk = b[1024:1280].reshape(4, 64).T
        put(("bk", o), np.concatenate([bk, bk], axis=0))
        put(("bo", o), _colvec(inp["swa_b_o"][o]))
        put(("sinks", o), np.broadcast_to(np.asarray(inp["swa_sinks"][o], np.float32)[None, :], (128, 16)))
    rows = np.zeros((1, 4 * 256), np.float32)
    for e in range(2):
        rows[0, e * 256:(e + 1) * 256] = inp["gla_b_f"][e]
    for o in range(2):
        rows[0, (2 + o) * 256:(3 + o) * 256] = np.asarray(inp["swa_b_qkv"][o])[1280:1536]
    q = np.arange(128)[:, None] + 128
    k = np.arange(256)[None, :]
    dist = q - k
    valid = (dist >= 0) & (dist < 128)
    bidx = _t5_bucket_np(np.maximum(dist, 0))
    rb = np.asarray(inp["rel_bias"], np.float32)
    tab = rb[bidx]
    tab = np.where(valid[:, :, None], tab, np.float32(NEG))
    bias_t = np.ascontiguousarray(tab.transpose(0, 2, 1))
    j = np.arange(128)[:, None]
    i = np.arange(128)[None, :]
    same = (j // 64) == (i // 64)
    cst = np.zeros((128, 5, 128), np.float32)
    cst[:, 0, :] = np.eye(128)
    cst[:, 1, :] = np.where(same & (j <= i), -1.0 / 16.0, 0.0)
    cst[:, 2, :] = np.where(same & (j > i), -1.0 / 16.0, 0.0)
    cst[:, 3, :] = np.where(same & (j <= i), 1.0, 0.0)
    cst[:, 4, :] = 1.0
    return vecs, rows, bias_t, cst


def build(S, T, TM, L=DEPTH):
    assert S % T == 0 and T % TM == 0 and TM % 128 == 0 and T % 256 == 0
    NPASS = S // T
    NSUB = T // TM
    NTT = TM // 128
    NCH = TM // 64
    nc = bass.Bass("TRN2", target_bir_lowering=False)

    def din(name, shape):
        return nc.dram_tensor(name, list(shape), F32, kind="ExternalInput").ap()

    x_d = din("x", [S, D])
    p_d = din("p", [DEPTH, S, PLE])
    vecs_d = din("vecs", [128, VM["_n"]])
    rows_d = din("rows", [1, 1024])
    bias_d = din("bias_t", [128, 16, 256])
    cst_d = din("cst", [128, 5, 128])
    bvb_d = din("bvb", [128, 2, 256])
    W = {}
    for nm in ("ffn1", "ffn2"):
        W[nm + "_w_gate"] = din(nm + "_w_gate", [DEPTH, D, DFF])
        W[nm + "_w_up"] = din(nm + "_w_up", [DEPTH, D, DFF])
        W[nm + "_w_down"] = din(nm + "_w_down", [DEPTH, DFF, D])
    W["ple_w_proj"] = din("ple_w_proj", [DEPTH, PLE, D])
    W["ple_w_gate"] = din("ple_w_gate", [DEPTH, D, D])
    W["hyb_w_in"] = din("hyb_w_in", [2, D, HYB_IN])
    W["hyb_w_out"] = din("hyb_w_out", [2, D, D])
    W["gla_w_fup"] = din("gla_w_fup", [2, 16, 256])
    W["lru_w_a"] = din("lru_w_a", [2, 8, 64, 64])
    W["lru_w_x"] = din("lru_w_x", [2, 8, 64, 64])
    W["swa_w_qkv"] = din("swa_w_qkv", [2, D, 1536])
    W["swa_w_o"] = din("swa_w_o", [2, D, D])
    out_d = nc.dram_tensor("out", [S, D], F32, kind="ExternalOutput").ap()

    p = Prog()
    st = contextlib.ExitStack()
    with st:
        def sb(name, shape, dt=F32):
            return st.enter_context(nc.sbuf_tensor("sb_" + name, list(shape), dt))

        ps = [st.enter_context(nc.psum_tensor(f"ps{i}", [128, 512], F32)) for i in range(8)]

        def psb(i):
            return ps[i][:].bitcast(BF16)

        def pk(i):
            return ("ps", i)

        xT = sb("xT", [128, KC, T])
        hT = sb("hT", [128, KC, T], BF16)
        vecs = sb("vecs", [128, VM["_n"]])
        rows = sb("rows", [1, 1024])
        bias_t = sb("bias_t", [128, 16, 256], BF16)
        bvb = sb("bvb", [128, 2, 256])
        cst = sb("cst", [128, 5, 128])
        cst_bf = sb("cst_bf", [128, 5, 128], BF16)
        ident, TriC, UpC, ones = cst[:, 0, :], cst[:, 1, :], cst[:, 2, :], cst[:, 4, :]
        ident_bf, maskA_bf, ones_bf = cst_bf[:, 0, :], cst_bf[:, 3, :], cst_bf[:, 4, :]
        dvec = sb("dvec", [128, 64])
        barc = sb("barc", [128, 2])
        NSLOT = 2
        wg = [sb(f"wg{i}", [128, KC, 256], BF16) for i in range(NSLOT)]
        wu = [sb(f"wu{i}", [128, KC, 256], BF16) for i in range(NSLOT)]
        wd = [sb(f"wd{i}", [128, 2, D], BF16) for i in range(NSLOT)]
        WA = sb("WA", [128, KC * 1552], BF16)
        WB = sb("WB", [128, KC * D], BF16)
        WC = sb("WC", [128, 2048], BF16)
        WAv = WA[:].rearrange("q (c f) -> q c f", c=KC)
        WBv = WB[:].rearrange("q (c f) -> q c f", c=KC)
        WCv8 = WC[:].rearrange("q (c f) -> q c f", c=KC)
        WCv2 = WC[:].rearrange("q (c f) -> q c f", c=2)
        wabd = sb("wabd", [128, 4, 128], BF16)
        wxbd = sb("wxbd", [128, 4, 128], BF16)
        sqb = sb("sqb", [128, KC, 256], BF16)
        lnv = sb("lnv", [128, 256])
        rstd = sb("rstd", [128, 256])
        nset_main = [(sqb[:], lnv[:], rstd[:], "m")]
        mixT = sb("mixT", [128, KC, TM], BF16)
        Sst = [sb(f"Sst{e}", [128, 2, 128]) for e in range(2)]
        hst = [sb(f"hst{e}", [128, 4]) for e in range(2)]
        xrh = [sb(f"xrh{e}", [128, 4, 3]) for e in range(2)]
        kTh = [sb(f"kTh{o}", [128, 4, 128], BF16) for o in range(2)]
        vbh = [sb(f"vbh{o}", [128, 256], BF16) for o in range(2)]

        AR_BYTES = 53504
        arena = sb("arena", [128, AR_BYTES // 4])
        ar_off = [0]

        def ar_reset():
            ar_off[0] = 0

        def ar(shape, dt=F32):
            esz = 4 if dt == F32 else 2
            n = 1
            for s_ in shape[1:]:
                n *= s_
            nb = (n * esz + 31) // 32 * 32
            o4 = ar_off[0] // 4
            ar_off[0] += nb
            assert ar_off[0] <= AR_BYTES, (ar_off[0], AR_BYTES)
            v = arena[0:shape[0], o4:o4 + nb // 4]
            if dt != F32:
                v = v.bitcast(dt)
            v = v[:, 0:n]
            if len(shape) == 3:
                v = v.rearrange("q (a b) -> q a b", a=shape[1])
            elif len(shape) == 4:
                v = v.rearrange("q (a b c) -> q a b c", a=shape[1], b=shape[2])
            return v

        ar_reset()
        xtok = [ar([128, D]) for _ in range(2)]
        ostg = ar([128, KC, 256])
        ar_reset()
        sgb = [ar([128, 2, 256]) for _ in range(2)]
        actb = [ar([128, 2, 256], BF16) for _ in range(2)]
        nset_ffn = nset_main + [(ar([128, KC, 256], BF16), ar([128, 256]), ar([128, 256]), "f%d" % i_) for i_ in range(2)]
        ar_reset()
        ptok = ar([128, T // 128, PLE], BF16)
        pT = ar([128, 2, T], BF16)
        gsb = [ar([128, 256]) for _ in range(2)]
        tmpA = [ar([128, 256]) for _ in range(2)]
        nset_ple = nset_main + [(ar([128, KC, 256], BF16), ar([128, 256]), ar([128, 256]), "p%d" % i_) for i_ in range(2)]
        ar_reset()
        S_all = ar([128, NCH, 2, 128], BF16)
        flT = ar([16, TM])
        wfup = ar([16, 256])
        sp_tok = ar([128, NTT, 256])
        edl = ar([128, 256])
        k2 = ar([128, NTT, 256], BF16)
        v_tok = ar([128, NTT, 512], BF16)
        Eb = ar([128, 2, TM])
        Einv = ar([128, 2, TM])
        qdm = ar([128, 4, TM], BF16)
        kdT = ar([128, 2, TM], BF16)
        silr = ar([128, 4, TM], BF16)
        attm = ar([128, 4, 128], BF16)
        lsets = []
        for _ in range(4):
            lsets.append(dict(xrb=ar([128, TM + 3]), xc=ar([128, TM]), xcb=ar([128, TM], BF16), gel=ar([128, TM], BF16),
                              lt=[ar([128, TM]) for _ in range(4)]))
        ar_reset()
        qT = ar([128, 16, TM], BF16)
        kT = ar([128, 4, 128 + TM], BF16)
        vb = ar([128, NTT + 1, 256], BF16)
        ssets = []
        for _ in range(2):
            ssets.append(dict(s_sb=ar([128, 4, 256]), p_sb=ar([128, 4, 256], BF16), pTs=ar([128, 8, 128], BF16),
                              st4=ar([128, 8, 4])))
        o_sb = ar([128, 16, 64], BF16)
        nset_odd = nset_main + [(ar([128, KC, 256], BF16), ar([128, 256]), ar([128, 256]), "o%d" % i_) for i_ in range(2)]

        if SHOW_SBUF:
            print('sbuf bytes remaining', nc.sbuf_bytes_remaining)
        def mm(out, lhsT, rhs, start, stop, r, w):
            p.add("pe", lambda e: e.matmul(out, lhsT=lhsT, rhs=rhs, start=start, stop=stop), r=r, w=w)

        def tr(out, in_, idn, r, w):
            p.add("pe", lambda e: e.transpose(out, in_, idn), r=r, w=w)

        def act(out, in_, func, r, w, bias=None, scale=None, accum_out=None):
            kw = {}
            if bias is not None:
                kw["bias"] = bias
            if scale is not None:
                kw["scale"] = scale
            if accum_out is not None:
                kw["accum_out"] = accum_out
            p.add("act", lambda e: e.activation(out=out, in_=in_, func=func, **kw), r=r, w=w)

        def tt(out, in0, in1, op, r, w):
            p.add("dve", lambda e: e.tensor_tensor(out=out, in0=in0, in1=in1, op=op), r=r, w=w)

        def stt(out, in0, scalar, in1, op0, op1, r, w):
            p.add("dve", lambda e: e.scalar_tensor_tensor(out=out, in0=in0, scalar=scalar, in1=in1, op0=op0, op1=op1),
                  r=r, w=w)

        def ts(out, in0, s1, s2, op0, op1, r, w):
            p.add("dve", lambda e: e.tensor_scalar(out=out, in0=in0, scalar1=s1, scalar2=s2, op0=op0, op1=op1), r=r, w=w)

        def cp(out, in_, r, w):
            p.add("dve", lambda e: e.tensor_copy(out=out, in_=in_), r=r, w=w)

        def mset(ap, val, w):
            p.add("dve", lambda e: e.memset(ap, val), w=w)

        def dma(eng, out, in_, r, w, noextra=False):
            p.add(eng, lambda e: e.dma_start(out=out, in_=in_), r=r, w=w, dma=True, noextra=noextra)

        def barrier():
            p.add("dve", lambda e: e.memset(barc[:], 0.0), w=["AR"], noextra=True)
            p.extra = ["AR"]

        def interleave(gens):
            gens = list(gens)
            while gens:
                for g_ in list(gens):
                    try:
                        next(g_)
                    except StopIteration:
                        gens.remove(g_)

        def xk(c, t0, n):
            return [("x", c, t) for t in range(t0 // 256, (t0 + n + 255) // 256)]

        def hk(c, t0, n):
            return [("h", c, t) for t in range(t0 // 256, (t0 + n + 255) // 256)]

        def hk_all(t0, n):
            return [k for c in range(KC) for k in hk(c, t0, n)]

        def V(key, n=1, off=0):
            c = VM[key] + off
            return vecs[:, c:c + n]

        dma("sp", vecs[:], vecs_d, [], ["vecs"])
        dma("sp", rows[:], rows_d, [], ["rows"])
        dma("sp", cst[:], cst_d, [], ["cst"])
        dma("pool", bias_t[:], bias_d, [], ["bias_t"])
        dma("sp", bvb[:], bvb_d, [], ["bvb"])
        cp(cst_bf[:], cst[:], ["cst"], ["cst_bf"])
        for e in range(2):
            mset(Sst[e][:], 0.0, [("Sst", e, 0), ("Sst", e, 1)])
            mset(hst[e][:], 0.0, [("hst", e, c) for c in range(4)])
            mset(xrh[e][:], 0.0, [("xrh", e, c) for c in range(4)])
        for o in range(2):
            mset(kTh[o][:], 0.0, [("kTh", o)])
            mset(vbh[o][:], 0.0, [("vbh", o)])
        mset(wabd[:], 0.0, [("wabd", g) for g in range(8)])
        mset(wxbd[:], 0.0, [("wxbd", g) for g in range(8)])
        for e in range(2):
            lam = V(("lru_lambda", e), 4)
            act(dvec[:, 40:44], lam, AF.Exp, ["vecs"], ["dv_t0"], scale=-1.0)
            act(dvec[:, 44:48], dvec[:, 40:44], AF.Ln, ["dv_t0"], ["dv_t1"], bias=1.0)
            ts(dvec[:, e * 8:e * 8 + 4], dvec[:, 44:48], -8.0, None, ALU.mult, ALU.bypass, ["dv_t1"], ["dvec"])
            ts(dvec[:, e * 8 + 4:e * 8 + 8], dvec[:, 44:48], -16.0, None, ALU.mult, ALU.bypass, ["dv_t1"], ["dvec"])
        for o in range(2):
            ts(dvec[:, 16 + o * 8:24 + o * 8], V(("bq", o), 8), 0.125, None, ALU.mult, ALU.bypass, ["vecs"], ["dvec"])

        def rmsnorm(gkey, dst, dst_keys_fn, nsets=None):
            nsets = nsets or nset_main
            for ti, a in enumerate(range(0, T, 256)):
                sqb_, lnv_, rstd_, tag = nsets[ti % len(nsets)]
                bank = 7 - (ti % len(nsets))
                xr_keys = [k for c in range(KC) for k in xk(c, a, 256)]
                act(sqb_, xT[:, :, a:a + 256], AF.Square, xr_keys, [("sqb", tag)])
                for c in range(KC):
                    mm(ps[bank][:, 0:256], ones_bf, sqb_[:, c, :], c == 0, c == KC - 1, [("sqb", tag), "cst_bf"], [pk(bank)])
                act(lnv_, ps[bank][:, 0:256], AF.Ln, [pk(bank)], [("lnv", tag)], bias=EPS, scale=1.0 / D)
                act(rstd_, lnv_, AF.Exp, [("lnv", tag)], [("rstd", tag)], scale=-0.5)
                for c in range(KC):
                    stt(dst[:, c, a:a + 256], xT[:, c, a:a + 256], V(gkey, 1, c), rstd_, ALU.mult, ALU.mult,
                        xk(c, a, 256) + [("rstd", tag), "vecs"], dst_keys_fn(c, a, 256))

        ffn_seq = []
        for _ in range(NPASS):
            for l in range(L):
                for nm in ("ffn1", "ffn2"):
                    for g in range(NG):
                        ffn_seq.append((nm, l, g))
        ffn_issued = [0]

        def ffn_issue_loads(upto):
            while ffn_issued[0] < min(upto, len(ffn_seq)):
                i = ffn_issued[0]
                nm, l, g = ffn_seq[i]
                s = i % NSLOT
                src_g = W[nm + "_w_gate"][l].rearrange("(c q) f -> q c f", q=128)[:, :, g * 256:(g + 1) * 256]
                src_u = W[nm + "_w_up"][l].rearrange("(c q) f -> q c f", q=128)[:, :, g * 256:(g + 1) * 256]
                src_d = W[nm + "_w_down"][l][g * 256:(g + 1) * 256, :].rearrange("(j q) d -> q j d", q=128)
                dma("pool", wg[s][:], src_g, [], [("wg", s)], noextra=True)
                dma("pool", wu[s][:], src_u, [], [("wu", s)], noextra=True)
                dma("pool", wd[s][:], src_d, [], [("wd", s)], noextra=True)
                ffn_issued[0] += 1

        ffn_pos = [0]

        def ffn(nm, l):
            barrier()
            rmsnorm((nm + "_norm", l), hT, hk, nset_ffn)
            stages = [(g, t) for g in range(NG) for t in range(T // 256)]
            base = ffn_pos[0]
            ffn_issue_loads(base + NSLOT)

            def GU(si):
                g, t = stages[si]
                s = (base + g) % NSLOT
                for j in range(2):
                    bank = 2 * (si % 2) + j
                    for k in range(KC):
                        mm(ps[bank][:, 0:256], wg[s][:, k, j * 128:(j + 1) * 128], hT[:, k, t * 256:(t + 1) * 256],
                           k == 0, k == KC - 1, [("wg", s), ("h", k, t)], [pk(bank)])
                    for k in range(KC):
                        mm(ps[bank][:, 256:512], wu[s][:, k, j * 128:(j + 1) * 128], hT[:, k, t * 256:(t + 1) * 256],
                           k == 0, k == KC - 1, [("wu", s), ("h", k, t)], [pk(bank)])

            def ACTV(si):
                b = si % 2
                for j in range(2):
                    bank = 2 * b + j
                    act(sgb[b][:, j, :], ps[bank][:, 0:256], AF.Silu, [pk(bank)], [("sgb", b, j)])
                    tt(actb[b][:, j, :], sgb[b][:, j, :], ps[bank][:, 256:512], ALU.mult,
                       [pk(bank), ("sgb", b, j)], [("actb", b, j)])

            def DN(si):
                g, t = stages[si]
                s = (base + g) % NSLOT
                b = si % 2
                for d in range(KC):
                    bank = 4 + d // 2
                    cs = (d % 2) * 256
                    for j in range(2):
                        mm(ps[bank][:, cs:cs + 256], wd[s][:, j, d * 128:(d + 1) * 128], actb[b][:, j, :],
                           j == 0, j == 1, [("wd", s), ("actb", b, j)], [pk(bank)])
                for d in range(KC):
                    bank = 4 + d // 2
                    cs = (d % 2) * 256
                    stt(xT[:, d, t * 256:(t + 1) * 256], ps[bank][:, cs:cs + 256], 0.5, xT[:, d, t * 256:(t + 1) * 256],
                        ALU.mult, ALU.add, [pk(bank)] + xk(d, t * 256, 256), xk(d, t * 256, 256))
                if t == T // 256 - 1:
                    ffn_issue_loads(base + g + NSLOT + 1)

            GU(0)
            for si in range(len(stages)):
                if si + 1 < len(stages):
                    GU(si + 1)
                ACTV(si)
                DN(si)
            ffn_pos[0] += NG

        def ple_loads(l):
            dma("pool", WBv, W["ple_w_gate"][l].rearrange("(c q) f -> q c f", q=128), [], ["WB"], noextra=True)
            dma("pool", WCv2, W["ple_w_proj"][l].rearrange("(c q) f -> q c f", q=128), [], ["WC"], noextra=True)

        def ple(l, tok0):
            barrier()
            dma("pool", ptok, p_d[l, tok0:tok0 + T, :].rearrange("(i q) c -> q i c", q=128), [], ["ptok"])
            for i in range(T // 128):
                for c in range(2):
                    tr(psb(6)[:, c * 128:(c + 1) * 128], ptok[:, i, c * 128:(c + 1) * 128], ident_bf,
                       ["ptok", "cst_bf"], [pk(6)])
                cp(pT[:, :, i * 128:(i + 1) * 128], psb(6)[:, 0:256].rearrange("q (c t) -> q c t", c=2),
                   [pk(6)], [("pT", i)])
            rmsnorm(("ple_norm", l), hT, hk, nset_ple)
            for a in range(0, T, 256):
                t = a // 256
                for d in range(KC):
                    ba, bb = (d % 2) * 2, (d % 2) * 2 + 1
                    for k in range(KC):
                        mm(ps[ba][:, 0:256], WBv[:, k, d * 128:(d + 1) * 128], hT[:, k, a:a + 256], k == 0, k == KC - 1,
                           ["WB", ("h", k, t)], [pk(ba)])
                    for k in range(2):
                        mm(ps[bb][:, 0:256], WCv2[:, k, d * 128:(d + 1) * 128], pT[:, k, a:a + 256], k == 0, k == 1,
                           ["WC", ("pT", 2 * t), ("pT", 2 * t + 1)], [pk(bb)])
                    gs_, tm_ = gsb[d % 2], tmpA[d % 2]
                    act(gs_, ps[ba][:, 0:256], AF.Sigmoid, [pk(ba)], [("gsb", d % 2)])
                    tt(tm_, gs_, ps[bb][:, 0:256], ALU.mult, [("gsb", d % 2), pk(bb)], [("tmpA", d % 2)])
                    tt(xT[:, d, a:a + 256], xT[:, d, a:a + 256], tm_, ALU.add, [("tmpA", d % 2)] + xk(d, a, 256), xk(d, a, 256))

        wa1_sub = [("WA1", kvh, dup) for kvh in range(4) for dup in range(2)]
        wa1_all = ["WA1"] + wa1_sub

        def even_loads_gla(e):
            win = W["hyb_w_in"][e].rearrange("(c q) f -> q c f", q=128)
            dma("pool", WAv[:, :, 0:1024], win[:, :, 0:1024], [], ["WA0"], noextra=True)
            dma("pool", WAv[:, :, 1024:1552], win[:, :, 1024:1552], [], wa1_all, noextra=True)
            dma("pool", WBv, W["hyb_w_out"][e].rearrange("(c q) f -> q c f", q=128), [], ["WB"], noextra=True)
            for g in range(8):
                c, h2 = g // 2, g % 2
                dma("pool", wabd[h2 * 64:(h2 + 1) * 64, c, h2 * 64:(h2 + 1) * 64], W["lru_w_a"][e, g], [], [("wabd", g)],
                    noextra=True)
                dma("pool", wxbd[h2 * 64:(h2 + 1) * 64, c, h2 * 64:(h2 + 1) * 64], W["lru_w_x"][e, g], [], [("wxbd", g)],
                    noextra=True)

        def even_loads_lru(e):
            win = W["hyb_w_in"][e].rearrange("(c q) f -> q c f", q=128)
            dma("pool", WAv[:, :, 0:1024], win[:, :, 1552:2576], [], ["WA0"], noextra=True)

        def even_mixer(l, tok0):
            e = l // 2
            barrier()
            rmsnorm(("mix_norm", l), hT, hk)
            dma("sp", wfup, W["gla_w_fup"][e], [], ["wfup"])
            S = Sst[e]
            for m in range(NSUB if "gla" in PARTS else 0):
                a0 = m * TM
                hkeys = hk_all(a0, TM)

                def proj(bank, col0, ncol, wkeys):
                    for k in range(KC):
                        mm(ps[bank][0:ncol, 0:TM], WAv[:, k, col0:col0 + ncol], hT[:, k, a0:a0 + TM], k == 0, k == KC - 1,
                           wkeys + hkeys, [pk(bank)])

                for h in range(4):
                    orow = slice((1 - h % 2) * 64, (1 - h % 2) * 64 + 64)
                    mset(qdm[orow, h, :], 0.0, [("qdm", h)])
                proj(0, 1536, 16, wa1_all)
                cp(flT, ps[0][0:16, 0:TM], [pk(0)], ["flT"])
                if GLA_STOP == 1:
                    continue
                for i in range(NTT):
                    tk = a0 + i * 128
                    mm(ps[1][:, 0:256], flT[0:16, i * 128:(i + 1) * 128], wfup[0:16, :], True, False, ["flT", "wfup"], [pk(1)])
                    mm(ps[1][:, 0:256], ones[0:1, :], rows[0:1, e * 256:(e + 1) * 256], False, True, ["cst", "rows"], [pk(1)])
                    act(edl, ps[1][:, 0:256], AF.Exp, [pk(1)], ["edl"], scale=-1.0)
                    act(sp_tok[:, i, :], edl, AF.Ln, ["edl"], [("sp_tok", i)], bias=1.0)
                    for hp in range(2):
                        mm(ps[2 + hp][:, i * 128:(i + 1) * 128], sp_tok[:, i, hp * 128:(hp + 1) * 128], TriC, True, True,
                           [("sp_tok", i), "cst"], [pk(2 + hp)])
                    mm(ps[4][:, 0:256], UpC, sp_tok[:, i, :], True, True, [("sp_tok", i), "cst"], [pk(4)])
                    act(edl, ps[4][:, 0:256], AF.Exp, [pk(4)], ["edl"])
                    for k in range(KC):
                        mm(ps[5][:, 0:256], hT[:, k, tk:tk + 128], WAv[:, k, 256:512], k == 0, k == KC - 1,
                           ["WA0"] + hkeys, [pk(5)])
                    tt(k2[:, i, :], ps[5][:, 0:256], edl, ALU.mult, [pk(5), "edl"], [("k2", i)])
                    for k in range(KC):
                        mm(ps[6][:, 0:512], hT[:, k, tk:tk + 128], WAv[:, k, 512:1024], k == 0, k == KC - 1,
                           ["WA0"] + hkeys, [pk(6)])
                    act(v_tok[:, i, :], ps[6][:, 0:512], AF.Copy, [pk(6)], [("v_tok", i)])
                if GLA_STOP == 2:
                    continue
                for hp in range(2):
                    act(Eb[:, hp, :], ps[2 + hp][:, 0:TM], AF.Exp, [pk(2 + hp)], [("Eb", hp)])
                    act(Einv[:, hp, :], ps[2 + hp][:, 0:TM], AF.Exp, [pk(2 + hp)], [("Einv", hp)], scale=-1.0)
                def gen_proj():
                    for hp in range(2):
                        proj(0, hp * 128, 128, ["WA0"])
                        for h2 in range(2):
                            pr = slice(h2 * 64, (h2 + 1) * 64)
                            stt(qdm[pr, 2 * hp + h2, :], ps[0][pr, 0:TM], 0.125, Eb[pr, hp, :], ALU.mult, ALU.mult,
                                [pk(0), ("Eb", hp)], [("qdm", 2 * hp + h2)])
                        yield
                        proj(1, 256 + hp * 128, 128, ["WA0"])
                        tt(kdT[:, hp, :], ps[1][:, 0:TM], Einv[:, hp, :], ALU.mult, [pk(1), ("Einv", hp)], [("kdT", hp)])
                        yield
                    for c in range(4):
                        proj(c % 2, 1024 + c * 128, 128, wa1_all)
                        act(silr[:, c, :], ps[c % 2][:, 0:TM], AF.Silu, [pk(c % 2)], [("silr", c)])
                        yield

                def gen_rec():
                    for n in range(NCH):
                        i, c2 = n // 2, n % 2
                        rsl = slice(c2 * 64, (c2 + 1) * 64)
                        for hp in range(2):
                            cp(S_all[:, n, hp, :], S[:, hp, :], [("Sst", e, hp)], [("S_all", n, hp)])
                            mm(ps[4 + hp][:, 0:256], k2[rsl, i, hp * 128:(hp + 1) * 128], v_tok[rsl, i, hp * 256:(hp + 1) * 256],
                               True, True, [("k2", i), ("v_tok", i)], [pk(4 + hp)])
                            yield
                            for h2 in range(2):
                                pr = slice(h2 * 64, (h2 + 1) * 64)
                                stt(S[pr, hp, :], S[pr, hp, :], Eb[pr, hp, n * 64 + 63:n * 64 + 64],
                                    ps[4 + hp][pr, h2 * 128:(h2 + 1) * 128], ALU.mult, ALU.add,
                                    [("Sst", e, hp), ("Eb", hp), pk(4 + hp)], [("Sst", e, hp)])
                            yield

                interleave([gen_proj(), gen_rec()])
                for i in range(NTT):
                    tsl = slice(i * 128, (i + 1) * 128)
                    for h in range(4):
                        hp, pr = h // 2, slice((h % 2) * 64, (h % 2) * 64 + 64)
                        mm(ps[6][:, h * 128:(h + 1) * 128], kdT[:, hp, tsl], qdm[:, h, tsl], True, True,
                           [("kdT", hp), ("qdm", h)], [pk(6)])
                    for h in range(4):
                        tt(attm[:, h, :], ps[6][:, h * 128:(h + 1) * 128], maskA_bf, ALU.mult, [pk(6), "cst_bf"], [("attm", h)])
                    for h in range(4):
                        hp, pr = h // 2, slice((h % 2) * 64, (h % 2) * 64 + 64)
                        mm(ps[h][:, tsl], v_tok[:, i, h * 128:(h + 1) * 128], attm[:, h, :], True, False,
                           [("v_tok", i), ("attm", h)], [pk(h)])
                        for c2 in range(2):
                            n = 2 * i + c2
                            csl = slice(i * 128 + c2 * 64, i * 128 + c2 * 64 + 64)
                            mm(ps[h][:, csl], S_all[:, n, hp, :], qdm[:, h, csl], False, c2 == 1,
                               [("S_all", n, hp), ("qdm", h)], [pk(h)])
                if GLA_STOP == 5:
                    continue
                def head_norm(h):
                    sqh, lnh, rsh, t1 = lsets[h]["lt"]
                    K_ = lambda nm: (nm, h)
                    nb = 4 + h
                    act(sqh, ps[h][:, 0:TM], AF.Square, [pk(h)], [K_("lt0")])
                    yield
                    mm(ps[nb][:, 0:TM], ones, sqh, True, True, [K_("lt0"), "cst"], [pk(nb)])
                    yield
                    act(lnh, ps[nb][:, 0:TM], AF.Ln, [pk(nb)], [K_("lt1")], bias=EPS, scale=1.0 / 128)
                    yield
                    act(rsh, lnh, AF.Exp, [K_("lt1")], [K_("lt2")], scale=-0.5)
                    yield
                    tt(t1, ps[h][:, 0:TM], rsh, ALU.mult, [pk(h), K_("lt2")], [K_("lt3")])
                    yield
                    stt(mixT[:, h, :], t1, V(("gla_norm", e), 1, h), silr[:, h, :], ALU.mult, ALU.mult,
                        [K_("lt3"), "vecs", ("silr", h)], [("mixT", h)])
                    yield

                interleave([head_norm(h) for h in range(4)])
                for d in range(KC):
                    bank = 4 + d % 2
                    for k in range(4):
                        mm(ps[bank][:, 0:TM], WBv[:, k, d * 128:(d + 1) * 128], mixT[:, k, :], k == 0, k == 3,
                           ["WB", ("mixT", k)], [pk(bank)])
                    tt(xT[:, d, a0:a0 + TM], xT[:, d, a0:a0 + TM], ps[bank][:, 0:TM], ALU.add,
                       [pk(bank)] + xk(d, a0, TM), xk(d, a0, TM))
            even_loads_lru(e)
            cneg = dvec[:, e * 8:e * 8 + 4]
            cneg2 = dvec[:, e * 8 + 4:e * 8 + 8]
            for m in range(NSUB if "lru" in PARTS else 0):
                a0 = m * TM
                hkeys = hk_all(a0, TM)
                def lru_block(c, si):
                    L_ = lsets[si]
                    xrb, xc, xcb, gel, lt = L_["xrb"], L_["xc"], L_["xcb"], L_["gel"], L_["lt"]
                    K_ = lambda nm: (nm, si)
                    b0, b1 = 2 * si, 2 * si + 1
                    for k in range(KC):
                        mm(ps[b0][:, 0:TM], WAv[:, k, c * 128:(c + 1) * 128], hT[:, k, a0:a0 + TM], k == 0, k == KC - 1,
                           ["WA0"] + hkeys, [pk(b0)])
                    cp(xrb[:, 0:3], xrh[e][:, c, :], [("xrh", e, c)], [K_("xrb")])
                    act(xrb[:, 3:3 + TM], ps[b0][:, 0:TM], AF.Copy, [pk(b0)], [K_("xrb")])
                    yield
                    for k in range(KC):
                        mm(ps[b1][:, 0:TM], WAv[:, k, 512 + c * 128:512 + (c + 1) * 128], hT[:, k, a0:a0 + TM],
                           k == 0, k == KC - 1, ["WA0"] + hkeys, [pk(b1)])
                    act(lt[0], ps[b1][:, 0:TM], AF.Square, [pk(b1)], [K_("lt0")])
                    yield
                    ts(lt[0], lt[0], 0.044715, 1.0, ALU.mult, ALU.add, [K_("lt0")], [K_("lt0")])
                    tt(lt[0], lt[0], ps[b1][:, 0:TM], ALU.mult, [K_("lt0"), pk(b1)], [K_("lt0")])
                    yield
                    act(lt[1], lt[0], AF.Sigmoid, [K_("lt0")], [K_("lt1")], scale=1.5957691216057308)
                    yield
                    tt(gel, lt[1], ps[b1][:, 0:TM], ALU.mult, [K_("lt1"), pk(b1)], [K_("gel")])
                    cw = VM[("conv_w", e)] + c * 4
                    ts(xc, xrb[:, 0:TM], vecs[:, cw:cw + 1], V(("conv_b", e), 1, c), ALU.mult, ALU.add,
                       [K_("xrb"), "vecs"], [K_("xc")])
                    yield
                    for tap in range(1, 4):
                        stt(xc, xrb[:, tap:tap + TM], vecs[:, cw + tap:cw + tap + 1], xc, ALU.mult, ALU.add,
                            [K_("xrb"), K_("xc"), "vecs"], [K_("xc")])
                        yield
                    cp(xrh[e][:, c, :], xrb[:, TM:TM + 3], [K_("xrb")], [("xrh", e, c)])
                    act(xcb, xc, AF.Copy, [K_("xc")], [K_("xcb")])
                    yield
                    mm(ps[b0][:, 0:TM], wabd[:, c, :], xcb, True, True, [("wabd", 2 * c), ("wabd", 2 * c + 1), K_("xcb")], [pk(b0)])
                    mm(ps[b1][:, 0:TM], wxbd[:, c, :], xcb, True, True, [("wxbd", 2 * c), ("wxbd", 2 * c + 1), K_("xcb")], [pk(b1)])
                    act(lt[2], ps[b0][:, 0:TM], AF.Sigmoid, [pk(b0), "vecs"], [K_("lt2")], bias=V(("lru_b_a", e), 1, c))
                    act(lt[3], ps[b1][:, 0:TM], AF.Sigmoid, [pk(b1), "vecs"], [K_("lt3")], bias=V(("lru_b_x", e), 1, c))
                    yield
                    act(lt[0], lt[2], AF.Exp, [K_("lt2"), "dvec"], [K_("lt0")], scale=cneg[:, c:c + 1])
                    act(lt[1], lt[2], AF.Exp, [K_("lt2"), "dvec"], [K_("lt1")], scale=cneg2[:, c:c + 1])
                    tt(lt[3], lt[3], xc, ALU.mult, [K_("lt3"), K_("xc")], [K_("lt3")])
                    yield
                    ts(lt[1], lt[1], -1.0, 1.0, ALU.mult, ALU.add, [K_("lt1")], [K_("lt1")])
                    yield
                    act(lt[1], lt[1], AF.Sqrt, [K_("lt1")], [K_("lt1")])
                    yield
                    tt(lt[3], lt[3], lt[1], ALU.mult, [K_("lt3"), K_("lt1")], [K_("lt3")])
                    yield
                    p.add("dve", lambda eng: eng.tensor_tensor_scan(out=lt[2], data0=lt[0], data1=lt[3],
                                                                    initial=hst[e][:, c:c + 1], op0=ALU.mult, op1=ALU.add),
                          r=[K_("lt0"), K_("lt3"), ("hst", e, c)], w=[K_("lt2")])
                    yield
                    cp(hst[e][:, c:c + 1], lt[2][:, TM - 1:TM], [K_("lt2")], [("hst", e, c)])
                    tt(mixT[:, 4 + c, :], lt[2], gel, ALU.mult, [K_("lt2"), K_("gel")], [("mixT", 4 + c)])
                    yield

                interleave([lru_block(c, c) for c in range(4)])
                for d in range(KC):
                    bank = 4 + d % 2
                    for k in range(4):
                        mm(ps[bank][:, 0:TM], WBv[:, 4 + k, d * 128:(d + 1) * 128], mixT[:, 4 + k, :], k == 0, k == 3,
                           ["WB", ("mixT", 4 + k)], [pk(bank)])
                    tt(xT[:, d, a0:a0 + TM], xT[:, d, a0:a0 + TM], ps[bank][:, 0:TM], ALU.add,
                       [pk(bank)] + xk(d, a0, TM), xk(d, a0, TM))

        def odd_loads(o):
            wqkv = W["swa_w_qkv"][o].rearrange("(c q) f -> q c f", q=128)
            dma("pool", WAv[:, :, 0:1024], wqkv[:, :, 0:1024], [], ["WA0"], noextra=True)
            for kvh in range(4):
                for dup in range(2):
                    c0 = 1024 + kvh * 128 + dup * 64
                    dma("pool", WAv[:, :, c0:c0 + 64], wqkv[:, :, 1024 + kvh * 64:1024 + kvh * 64 + 64], [],
                        [("WA1", kvh, dup)], noextra=True)
            dma("pool", WBv, W["swa_w_o"][o].rearrange("(c q) f -> q c f", q=128), [], ["WB"], noextra=True)
            dma("pool", WCv8, wqkv[:, :, 1280:1536], [], ["WC"], noextra=True)

        def odd_mixer(l, tok0):
            o = l // 2
            barrier()
            rmsnorm(("mix_norm", l), hT, hk, nset_odd)
            for m in range(NSUB):
                a0 = m * TM
                hkeys = hk_all(a0, TM)
                for blk in range(8):
                    bank = 6 + blk % 2
                    for k in range(KC):
                        mm(ps[bank][:, 0:TM], WAv[:, k, blk * 128:(blk + 1) * 128], hT[:, k, a0:a0 + TM], k == 0, k == KC - 1,
                           ["WA0"] + hkeys, [pk(bank)])
                    for h2 in range(2):
                        pr = slice(h2 * 64, (h2 + 1) * 64)
                        orow = slice((1 - h2) * 64, (1 - h2) * 64 + 64)
                        mset(qT[orow, 2 * blk + h2, :], 0.0, [("qT", 2 * blk + h2)])
                        act(qT[pr, 2 * blk + h2, :], ps[bank][pr, 0:TM], AF.Identity, [pk(bank), "dvec"],
                            [("qT", 2 * blk + h2)], bias=dvec[pr, 16 + o * 8 + blk:16 + o * 8 + blk + 1], scale=0.125)
                if ODD_STOP == 1:
                    continue
                cp(kT[:, :, 0:128], kTh[o][:], [("kTh", o)], ["kT"])
                cp(vb[:, 0, :], vbh[o][:], [("vbh", o)], [("vb", 0)])
                for kvh in range(4):
                    bank = 6 + kvh % 2
                    for k in range(KC):
                        mm(ps[bank][:, 0:TM], WAv[:, k, 1024 + kvh * 128:1024 + (kvh + 1) * 128], hT[:, k, a0:a0 + TM],
                           k == 0, k == KC - 1, [("WA1", kvh, 0), ("WA1", kvh, 1)] + hkeys, [pk(bank)])
                    act(kT[:, kvh, 128:128 + TM], ps[bank][:, 0:TM], AF.Identity, [pk(bank), "vecs"], ["kT"],
                        bias=V(("bk", o), 1, kvh))
                for i in range(NTT):
                    tk = a0 + i * 128
                    bank = 6 + i % 2
                    for k in range(KC):
                        mm(ps[bank][:, 0:256], hT[:, k, tk:tk + 128], WCv8[:, k, :], k == 0, k == KC - 1, ["WC"] + hkeys,
                           [pk(bank)])
                    tt(vb[:, 1 + i, :], ps[bank][:, 0:256], bvb[:, o, :], ALU.add, [pk(bank), "bvb"], [("vb", 1 + i)])
                cp(kTh[o][:], kT[:, :, TM:TM + 128], ["kT"], [("kTh", o)])
                cp(vbh[o][:], vb[:, NTT, :], [("vb", NTT)], [("vbh", o)])
                if ODD_STOP == 2:
                    continue
                for i in range(NTT):
                    gb = (tok0 + a0) // 128 + i
                    qsl = slice(i * 128, (i + 1) * 128)
                    def swa_kvh(kvh, si):
                        S_ = ssets[si]
                        s_sb, p_sb, pTs, st4 = S_["s_sb"], S_["p_sb"], S_["pTs"], S_["st4"]
                        negmx, rsum, t4, es, den, rden = (st4[:, ii, :] for ii in range(6))
                        K_ = lambda nm: (nm, si)
                        sb0 = (0, 1) if si == 0 else (6, 7)
                        tb = 2 if si == 0 else 5
                        k0 = 128 if gb == 0 else 0
                        for g in range(4):
                            h = 4 * kvh + g
                            bk_ = sb0[g // 2]
                            reg = ps[bk_][:, (g % 2) * 256:(g % 2) * 256 + 256]
                            mm(reg, qT[:, h, qsl], kT[:, kvh, i * 128:i * 128 + 256], True, False, [("qT", h), "kT"], [pk(bk_)])
                            mm(reg, ident_bf, bias_t[:, h, :], False, True, ["cst_bf", "bias_t"], [pk(bk_)])
                        yield
                        for hb in range(2):
                            sv = ps[sb0[hb]][:, 0:512].rearrange("q (g t) -> q g t", g=2)[:, :, k0:256]
                            p.add("dve", lambda eng, sv=sv, hb=hb: eng.tensor_reduce(out=negmx[:, 2 * hb:2 * hb + 2], in_=sv, axis=AX.X,
                                                                                   op=ALU.max, negate=True),
                                  r=[pk(sb0[hb])], w=[K_("negmx")])
                        if gb == 0:
                            mset(p_sb[:, :, 0:128], 0.0, [("p_sb", si, g) for g in range(4)])
                        yield
                        for g in range(4):
                            bk_ = sb0[g // 2]
                            act(p_sb[:, g, k0:256], ps[bk_][:, (g % 2) * 256 + k0:(g % 2) * 256 + 256], AF.Exp,
                                [pk(bk_), K_("negmx")], [("p_sb", si, g), ("rsum", si, g)],
                                bias=negmx[:, g:g + 1], accum_out=rsum[:, g:g + 1])
                        tt(t4, negmx, V(("sinks", o), 4, 4 * kvh), ALU.add, [K_("negmx"), "vecs"], [K_("t4")])
                        yield
                        act(es, t4, AF.Exp, [K_("t4")], [K_("es")])
                        for g in range(4):
                            for half in range(2):
                                tr(psb(tb)[:, (g * 2 + half) * 128:(g * 2 + half + 1) * 128],
                                   p_sb[:, g, half * 128:(half + 1) * 128], ident_bf, [("p_sb", si, g), "cst_bf"], [pk(tb)])
                        yield
                        tt(den, rsum, es, ALU.add, [("rsum", si, g) for g in range(4)] + [K_("es")], [K_("den")])
                        cp(pTs, psb(tb)[:, 0:1024].rearrange("q (a t) -> q a t", a=8), [pk(tb)], [K_("pTs")])
                        yield
                        p.add("dve", lambda eng: eng.reciprocal(out=rden, in_=den), r=[K_("den")], w=[K_("rden")])
                        for g in range(4):
                            h = 4 * kvh + g
                            ob = 3 + h // 8
                            for half in range(2):
                                mm(ps[ob][:, (h % 8) * 64:(h % 8) * 64 + 64], pTs[:, g * 2 + half, :],
                                   vb[:, i + half, kvh * 64:(kvh + 1) * 64], half == 0, half == 1,
                                   [K_("pTs"), ("vb", i + half)], [pk(ob)])
                        yield
                        ob = 3 + kvh // 2
                        tt(o_sb[:, 4 * kvh:4 * kvh + 4, :],
                           ps[ob][:, (kvh % 2) * 256:(kvh % 2) * 256 + 256].rearrange("q (g d) -> q g d", g=4),
                           rden.unsqueeze(2).to_broadcast([128, 4, 64]), ALU.mult, [pk(ob), K_("rden")],
                           [("o_sb", 2 * kvh), ("o_sb", 2 * kvh + 1)])
                        yield

                    for kpair in ((0, 2), (1, 3)):
                        interleave([swa_kvh(kpair[0], 0), swa_kvh(kpair[1], 1)])
                    if ODD_STOP == 7:
                        continue
                    for b2 in range(8):
                        tr(psb(2)[:, b2 * 128:(b2 + 1) * 128], o_sb[:, 2 * b2:2 * b2 + 2, :].rearrange("q a d -> q (a d)"),
                           ident_bf, [("o_sb", b2), "cst_bf"], [pk(2)])
                    cp(mixT[:, :, qsl], psb(2)[:, 0:1024].rearrange("q (a t) -> q a t", a=8), [pk(2)],
                       [("mixT", k) for k in range(8)])
                for d in range(KC):
                    bank = 6 + d % 2
                    for k in range(KC):
                        mm(ps[bank][:, 0:TM], WBv[:, k, d * 128:(d + 1) * 128], mixT[:, k, :], k == 0, k == KC - 1,
                           ["WB", ("mixT", k)], [pk(bank)])
                    stt(xT[:, d, a0:a0 + TM], ps[bank][:, 0:TM], V(("bo", o), 1, d), xT[:, d, a0:a0 + TM], ALU.add, ALU.add,
                        [pk(bank), "vecs"] + xk(d, a0, TM), xk(d, a0, TM))

        for ps_i in range(NPASS):
            tok0 = ps_i * T
            barrier()
            for i in range(T // 128):
                xb_ = xtok[i % 2]
                dma("sp", xb_, x_d[tok0 + i * 128:tok0 + (i + 1) * 128, :], [], [("xtok", i % 2)])
                for half in range(2):
                    for c in range(4):
                        tr(ps[half][:, c * 128:(c + 1) * 128], xb_[:, (half * 4 + c) * 128:(half * 4 + c + 1) * 128], ident,
                           [("xtok", i % 2), "cst"], [pk(half)])
                    cp(xT[:, half * 4:half * 4 + 4, i * 128:(i + 1) * 128],
                       ps[half][:, 0:512].rearrange("q (c t) -> q c t", c=4), [pk(half)],
                       [k for c in range(half * 4, half * 4 + 4) for k in xk(c, i * 128, 128)])
            for l in range(L):
                if "mix" in PARTS:
                    if l % 2 == 0:
                        even_loads_gla(l // 2)
                    else:
                        odd_loads(l // 2)
                if "ffn1" in PARTS:
                    ffn("ffn1", l)
                else:
                    ffn_pos[0] += NG
                    ffn_issued[0] = max(ffn_issued[0], ffn_pos[0])
                if "mix" in PARTS:
                    if l % 2 == 0:
                        even_mixer(l, tok0)
                    else:
                        odd_mixer(l, tok0)
                if "ple" in PARTS:
                    ple_loads(l)
                if "ffn2" in PARTS:
                    ffn("ffn2", l)
                else:
                    ffn_pos[0] += NG
                    ffn_issued[0] = max(ffn_issued[0], ffn_pos[0])
                if "ple" in PARTS:
                    ple(l, tok0)
            barrier()
            for a in range(0, T, 256):
                xr_keys = [k for c in range(KC) for k in xk(c, a, 256)]
                act(ostg, xT[:, :, a:a + 256], AF.Square, xr_keys, ["sq"])
                for c in range(KC):
                    mm(ps[7][:, 0:256], ones, ostg[:, c, :], c == 0, c == KC - 1, ["sq", "cst"], [pk(7)])
                act(lnv[:], ps[7][:, 0:256], AF.Ln, [pk(7)], ["lnv"], bias=EPS, scale=1.0 / D)
                act(rstd[:], lnv[:], AF.Exp, ["lnv"], ["rstd"], scale=-0.5)
                for c in range(KC):
                    stt(ostg[:, c, :], xT[:, c, a:a + 256], V("final_norm", 1, c), rstd[:], ALU.mult, ALU.mult,
                        xk(c, a, 256) + ["rstd", "vecs", "sq"], ["sq"])
                for i in range(2):
                    ob_ = xtok[i % 2]
                    for half in range(2):
                        for c in range(4):
                            tr(ps[half][:, c * 128:(c + 1) * 128], ostg[:, half * 4 + c, i * 128:(i + 1) * 128], ident,
                               ["sq", "cst"], [pk(half)])
                        cp(ob_[:, half * 512:(half + 1) * 512], ps[half][:, 0:512], [pk(half)], [("xtok", i % 2)])
                    r0 = tok0 + a + i * 128
                    dma("sp", out_d[r0:r0 + 128, :], ob_, [("xtok", i % 2)], [("out", r0)])
        p.add("sp", None, r=[("out", r0) for r0 in range(0, S, 128)])
        p.emit(nc)
    return nc


SHOW_SBUF = False
GLA_STOP = 0
ODD_STOP = 0
PARTS = {"ffn1", "mix", "ffn2", "ple", "gla", "lru"}
S_FULL = 4096
T_PASS = 1024
T_MIX = 256
N_CORES = 4

_W_NAMES = ["ffn1_w_gate", "ffn1_w_up", "ffn1_w_down", "ffn2_w_gate", "ffn2_w_up", "ffn2_w_down",
            "ple_w_proj", "ple_w_gate", "hyb_w_in", "hyb_w_out", "gla_w_fup", "lru_w_a", "lru_w_x",
            "swa_w_qkv", "swa_w_o"]


def make_in_maps(inp, n_seq, S):
    vecs, rows, bias_t, cst = host_layout(inp)
    bvb = np.ascontiguousarray(np.broadcast_to(rows[0, 512:1024].reshape(1, 2, 256), (128, 2, 256)))
    shared = {"vecs": vecs, "rows": rows, "bias_t": bias_t, "cst": cst, "bvb": bvb}
    for nm in _W_NAMES:
        shared[nm] = np.ascontiguousarray(np.asarray(inp[nm], np.float32))
    maps = []
    for b in range(n_seq):
        m = dict(shared)
        m["x"] = np.ascontiguousarray(np.asarray(inp["x"][b, :S], np.float32))
        m["p"] = np.ascontiguousarray(np.asarray(inp["p"][:, b, :S], np.float32))
        maps.append(m)
    return maps


def kernel(**inputs):
    nc = build(S_FULL, T_PASS, T_MIX)
    maps = make_in_maps(inputs, N_CORES, S_FULL)
    res = run_bass_kernel_spmd(nc, maps, core_ids=list(range(N_CORES)))
    return np.stack([np.asarray(r["out"], np.float32) for r in res.results], axis=0)
```
